# Optimizing a Trainium2 kernel written in Bass

```python
import jax, jax.numpy as jnp
from jax import lax
import numpy as np

D_MODEL = 4096
BATCH = 1
SEQ = 8192
DEPTH = 1

MIX_WIDTH = D_MODEL
D_A = MIX_WIDTH // 2
D_B = MIX_WIDTH - D_A
HEAD_DIM_A = 128
N_HEADS_A = D_A // HEAD_DIM_A
CHUNK = 128
CONV_W = 3
N_CONV_GROUPS = 16
IN_COLS = 2 * D_A + 3 * D_B
D_FF = ((8 * D_MODEL // 3 + 255) // 256) * 256
EPS = 1e-6

kernel_name = "hybrid_gmlp_shortconv_layer"


def rmsnorm(x, g):
    xf = x.astype(jnp.float32)
    y = xf * lax.rsqrt(jnp.mean(xf * xf, axis=-1, keepdims=True) + EPS)
    return y.astype(x.dtype) * g


def head_layernorm(v, g, b):
    vf = v.astype(jnp.float32)
    mu = jnp.mean(vf, axis=-1, keepdims=True)
    var = jnp.mean(jnp.square(vf - mu), axis=-1, keepdims=True)
    y = ((vf - mu) * lax.rsqrt(var + EPS)).astype(v.dtype)
    return y * g.reshape(N_HEADS_A, HEAD_DIM_A) + b.reshape(N_HEADS_A, HEAD_DIM_A)


def spatial_gating(u, v, ln_g, ln_b, w_s, b_s):
    bsz, seq, _ = v.shape
    v = head_layernorm(v.reshape(bsz, seq, N_HEADS_A, HEAD_DIM_A), ln_g, ln_b)
    vc = v.reshape(bsz, seq // CHUNK, CHUNK, N_HEADS_A, HEAD_DIM_A)
    mask = jnp.tril(jnp.ones((CHUNK, CHUNK), dtype=bool))
    ws = jnp.where(mask[None], w_s, jnp.zeros((), w_s.dtype))
    mixed = jnp.einsum('hts,bnshd->bnthd', ws, vc) + b_s.T[None, None, :, :, None]
    return u * mixed.reshape(bsz, seq, D_A)


def short_gated_conv(bg, cg, xb, conv_w):
    z = cg * xb
    y = lax.conv_general_dilated(
        z, conv_w[:, None, :].astype(z.dtype),
        window_strides=(1,), padding=[(CONV_W - 1, 0)],
        dimension_numbers=('NWC', 'WIO', 'NWC'),
        feature_group_count=D_B)
    return bg * y


def setup_inputs(seed: int = 0) -> dict:
    key = jax.random.key(seed)
    ks = jax.random.split(key, 18)
    f32 = jnp.float32
    nrm = lambda k, shape, s: jax.random.normal(k, shape, f32) * s
    L = DEPTH
    return {
        "x": jax.random.normal(ks[0], (BATCH, SEQ, D_MODEL), f32),
        "mix_norm_g": 1.0 + nrm(ks[1], (L, D_MODEL), 0.02),
        "w_in": nrm(ks[2], (L, D_MODEL, IN_COLS), D_MODEL ** -0.5),
        "ln_v_g": 1.0 + nrm(ks[3], (L, D_A), 0.02),
        "ln_v_b": nrm(ks[4], (L, D_A), 0.02),
        "w_spatial": nrm(ks[5], (L, N_HEADS_A, CHUNK, CHUNK), 0.5 * CHUNK ** -0.5),
        "b_spatial": 1.0 + nrm(ks[6], (L, N_HEADS_A, CHUNK), 0.02),
        "conv_w": nrm(ks[7], (L, CONV_W, D_B), CONV_W ** -0.5),
        "out_norm_a_g": 1.0 + nrm(ks[8], (L, D_A), 0.02),
        "out_norm_b_g": 1.0 + nrm(ks[9], (L, D_B), 0.02),
        "w_out": nrm(ks[10], (L, MIX_WIDTH, D_MODEL), MIX_WIDTH ** -0.5),
        "ffn_norm_g": 1.0 + nrm(ks[11], (L, D_MODEL), 0.02),
        "w_gate": nrm(ks[12], (L, D_MODEL, D_FF), D_MODEL ** -0.5),
        "w_up": nrm(ks[13], (L, D_MODEL, D_FF), D_MODEL ** -0.5),
        "w_down": nrm(ks[14], (L, D_FF, D_MODEL), D_FF ** -0.5),
        "final_norm_g": 1.0 + nrm(ks[15], (D_MODEL,), 0.02),
    }


def reference(x, mix_norm_g, w_in, ln_v_g, ln_v_b, w_spatial, b_spatial, conv_w,
              out_norm_a_g, out_norm_b_g, w_out, ffn_norm_g, w_gate, w_up, w_down,
              final_norm_g):
    for l in range(DEPTH):
        h = rmsnorm(x, mix_norm_g[l])
        p = jnp.einsum('bsd,dc->bsc', h, w_in[l])
        u_a, v_a, bg, cg, xb = jnp.split(
            p, [D_A, 2 * D_A, 2 * D_A + D_B, 2 * D_A + 2 * D_B], axis=-1)
        y_a = spatial_gating(jax.nn.gelu(u_a), jax.nn.gelu(v_a), ln_v_g[l], ln_v_b[l],
                             w_spatial[l], b_spatial[l])
        y_b = short_gated_conv(bg, cg, xb, conv_w[l])
        y = jnp.concatenate([rmsnorm(y_a, out_norm_a_g[l]),
                             rmsnorm(y_b, out_norm_b_g[l])], axis=-1)
        x = x + jnp.einsum('bsc,cd->bsd', y, w_out[l])
        h = rmsnorm(x, ffn_norm_g[l])
        g = jnp.einsum('bsd,df->bsf', h, w_gate[l])
        up = jnp.einsum('bsd,df->bsf', h, w_up[l])
        x = x + jnp.einsum('bsf,fd->bsd', jax.nn.silu(g) * up, w_down[l])
    return rmsnorm(x, final_norm_g)
```

```python
import numpy as np
from contextlib import ExitStack

import concourse.bass as bass
import concourse.mybir as mybir
from concourse.bass_utils import run_bass_kernel_spmd

F32 = mybir.dt.float32
BF16 = mybir.dt.bfloat16
AF = mybir.ActivationFunctionType
ALU = mybir.AluOpType
AX = mybir.AxisListType
EPS = 1e-6


class Cfg:
    def __init__(self, D=4096, NHA=16, NCB=16, F=11008, T=512, NTILE=2, SC=4, KP=8, R=3,
                 NCORES=8):
        self.D, self.NHA, self.NCB, self.F, self.T, self.NTILE = D, NHA, NCB, F, T, NTILE
        self.SC, self.KP, self.R, self.NCORES = SC, KP, R, NCORES
        self.DC = D // 128
        self.TC = T // 128
        self.FC = F // 128
        self.DA, self.DB = NHA * 128, NCB * 128
        self.MIXC = NHA + NCB
        self.MIX = self.MIXC * 128
        self.INC = 2 * self.DA + 3 * self.DB
        self.SW = SC * 128
        self.NPQ = self.DC // KP
        self.G = self.MIXC // 4
        self.DS = D // 512 if D >= 512 else 1
        self.DSW = min(512, D)
        assert self.MIXC * T == 4 * D, "YT aliasing (hn0|hn1|GB) needs MIXC*T == 4*D"
        assert self.DC % KP == 0 and NHA % KP == 0 and NCB % KP == 0
        assert NHA % SC == 0 and NCB % SC == 0 and SC % 2 == 0
        assert self.G <= KP and self.G % SC == 0
        assert self.TC * self.DA == 2 * KP * self.SW or True
        o = 0
        self.c_lnvg = o; o += NHA
        self.c_lnvb = o; o += NHA
        self.c_ga = o; o += NHA
        self.c_gb = o; o += NCB
        self.c_cw = o; o += 3 * NCB
        self.c_gmix = o; o += self.DC
        self.NCOLS = o


class _Op:
    __slots__ = ("eng", "emit", "deps", "raw", "is_dma", "key", "done", "n_dma")


class Sched:
    ENGS = ("pe", "act", "dve", "pool", "sp")

    def __init__(self):
        self.ops = []
        self.lastw = {}
        self.readers = {}
        self.cnt = {e: 0 for e in self.ENGS}
        self.dcnt = {}

    def add(self, eng, emit, reads=(), writes=(), dma_key=None, n_dma=1):
        idx = len(self.ops)
        deps, raw = set(), set()
        for r in reads:
            if r in self.lastw:
                deps.add(self.lastw[r]); raw.add(self.lastw[r])
            if r[0] == "P":
                for rd in self.readers.get(r, ()):
                    if self.ops[rd].eng != eng:
                        deps.add(rd)
        for w in writes:
            if w in self.lastw:
                deps.add(self.lastw[w]); raw.add(self.lastw[w])
            deps.update(self.readers.get(w, ()))
        op = _Op()
        op.eng, op.emit, op.is_dma, op.key, op.n_dma = eng, emit, dma_key is not None, dma_key, n_dma
        keep = set()
        for d in deps:
            dop = self.ops[d]
            if dop.is_dma:
                keep.add(d)
            elif dop.eng == eng:
                if eng != "pe" and d in raw:
                    keep.add(d)
            else:
                keep.add(d)
        op.deps = keep
        if op.is_dma:
            self.dcnt[dma_key] = self.dcnt.get(dma_key, 0) + n_dma
            op.done = ("d:" + dma_key, 16 * self.dcnt[dma_key])
        else:
            self.cnt[eng] += 1
            op.done = ("e:" + eng, self.cnt[eng])
        for r in reads:
            self.readers.setdefault(r, []).append(idx)
        for w in writes:
            self.lastw[w] = idx
            self.readers[w] = []
        self.ops.append(op)
        return idx

    def sem_names(self):
        names = ["e:" + e for e in self.ENGS]
        names += ["d:" + k for k in self.dcnt]
        return names

    def emit(self, eng_name, eng, sems):
        known = {}
        for op in self.ops:
            if op.eng != eng_name:
                continue
            waits = {}
            for d in op.deps:
                s, v = self.ops[d].done
                if v > waits.get(s, 0):
                    waits[s] = v
            for s, v in waits.items():
                if known.get(s, 0) >= v:
                    continue
                eng.wait_ge(sems[s], v)
                known[s] = v
            res = op.emit(eng)
            if op.is_dma:
                assert len(res) == op.n_dma
                for ins in res:
                    ins.then_inc(sems[op.done[0]], 16)
            else:
                res.then_inc(sems[op.done[0]], 1)


def build_program(cfg):
    c = cfg
    D, T, TC, DC, SC, KP, SW, NPQ = c.D, c.T, c.TC, c.DC, c.SC, c.KP, c.SW, c.NPQ
    NHA, NCB, MIXC, G, FC = c.NHA, c.NCB, c.MIXC, c.G, c.FC
    ROWS = c.NTILE * T

    nc = bass.Bass("TRN2", target_bir_lowering=False)
    x_d = nc.dram_tensor("x", [ROWS, D], F32, kind="ExternalInput").ap()
    xh_d = nc.dram_tensor("xh", [128, DC, 2], F32, kind="ExternalInput").ap()
    cols_d = nc.dram_tensor("cols", [128, c.NCOLS], F32, kind="ExternalInput").ap()
    gains_d = nc.dram_tensor("gains", [3, D], F32, kind="ExternalInput").ap()
    wsp_d = nc.dram_tensor("w_spatial", [NHA, 128, 128], F32, kind="ExternalInput").ap()
    bsp_d = nc.dram_tensor("b_spatial", [1, NHA * 128], F32, kind="ExternalInput").ap()
    win_d = nc.dram_tensor("w_in", [D, c.INC], F32, kind="ExternalInput").ap()
    wout_d = nc.dram_tensor("w_out", [c.MIX, D], F32, kind="ExternalInput").ap()
    wg_d = nc.dram_tensor("w_gate", [D, c.F], F32, kind="ExternalInput").ap()
    wu_d = nc.dram_tensor("w_up", [D, c.F], F32, kind="ExternalInput").ap()
    wd_d = nc.dram_tensor("w_down", [c.F, D], F32, kind="ExternalInput").ap()
    out_d = nc.dram_tensor("out", [ROWS, D], F32, kind="ExternalOutput").ap()

    win_v = win_d.rearrange("(dc p) c -> p dc c", p=128)
    wout_v = wout_d.rearrange("(cc p) d -> p cc d", p=128)
    wg_v = wg_d.rearrange("(dc p) f -> p dc f", p=128)
    wu_v = wu_d.rearrange("(dc p) f -> p dc f", p=128)
    wd_v = wd_d.rearrange("(fc p) d -> p fc d", p=128)

    es = ExitStack()
    with es:
        def sb(name, shape, dt):
            return es.enter_context(nc.sbuf_tensor(name, shape, dt))

        XRES = sb("XRES", [128, TC, D], F32)
        HTF = sb("HT", [128, (DC * T) // 2], F32)
        YTf = sb("YT", [128, MIXC * T], BF16)
        VLNf = sb("VLN", [128, TC * c.DA], BF16)
        SLOT = KP * max(SW, c.DSW)
        RING = sb("RING", [128, c.R, SLOT], BF16)
        TMP = sb("TMP", [128, 6, SW], F32)
        ZC = sb("ZC", [128, SC, T + 2], F32)
        YSQ = sb("YSQ", [128, SC, T], BF16)
        IDENT = sb("IDENT", [128, 128], BF16)
        ONESB = sb("ONESB", [128, 128], BF16)
        WST = sb("WST", [128, NHA, 128], BF16)
        CB = sb("CB", [128, NHA, 128], F32)
        COLS = sb("COLS", [128, c.NCOLS], F32)
        ST = sb("ST", [128, 320], F32)
        XH = sb("XH", [128, DC, 2], F32)
        XHS = sb("XHS", [128, DC, 2], F32)
        HTH = sb("HTH", [128, DC, 2], BF16)
        ZSAVE = sb("ZSAVE", [128, NCB, 2], F32)
        CH = sb("CH", [128, SC, 2], F32)
        EPSC = sb("EPSC", [128, 1], F32)
        PS = es.enter_context(nc.psum_tensor("PS", [128, 8, 512], F32))

        YT = YTf[:].rearrange("p (c t) -> p c t", t=T)
        HN = [YTf[:, 0:D], YTf[:, D:2 * D]]
        GB = YTf[:, 2 * D:4 * D].bitcast(F32)
        AT = [YTf[:, 0:G * T].rearrange("p (g t) -> p g t", t=T),
              YTf[:, G * T:2 * G * T].rearrange("p (g t) -> p g t", t=T)]
        VLN = VLNf[:].rearrange("p (tc d) -> p tc d", d=c.DA)
        SG = ZC[:, :, 0:T]
        IDF = HTF[:, 0:128]
        ONESF = ZC[:].rearrange("p a b -> p (a b)")[:, 0:128]
        HTflat = HTF[:].bitcast(BF16)
        HT = HTflat.rearrange("p (dc t) -> p dc t", t=T)

        def yres(ch):
            q4 = MIXC // 4
            return "YA" if ch < q4 else ("YB" if ch < 2 * q4 else "YC")

        HNRES = ["YA", "YB"]
        ATRES = ["YA", "YB"]

        slot_elems = SLOT
        n_extra = (TC * c.DA) // slot_elems
        n_extra = min(n_extra, 2)
        slots = [RING[:, i, :] for i in range(c.R)] + \
                [VLNf[:, i * slot_elems:(i + 1) * slot_elems] for i in range(n_extra)]
        slot_res = ["S%d" % i for i in range(c.R)] + ["V%d" % i for i in range(n_extra)]
        VLN_RES = ["V%d" % i for i in range(n_extra)] + ["VLNrest"]

        def bank(b):
            return PS[:, b, :]

        def bank_bf(b):
            return PS[:, b, :].bitcast(BF16).rearrange("p (j c) -> p j c", c=128)

        XRf = XRES[:].rearrange("p tc d -> p (tc d)")
        spt = max(1, (D * 2) // slot_elems)
        xs_f32 = (D // spt)
        assert xs_f32 * 2 >= slot_elems
        n_x = TC * spt
        x_slot_ids = list(range(len(slots), len(slots) + n_x))
        for j in range(n_x):
            slots.append(XRf[:, j * xs_f32:j * xs_f32 + slot_elems // 2].bitcast(BF16))
            slot_res.append("XS%d" % j)
        v_slot_ids = list(range(c.R, c.R + n_extra))
        base_ids = list(range(c.R))
        MODES = {"early": base_ids + x_slot_ids, "late": base_ids + v_slot_ids, "base": base_ids}

        def xr(tc):
            return ["XS%d" % (tc * spt + j) for j in range(spt)]

        sch = Sched()
        ring_state = {"i": 0}

        def next_slot(mode):
            ids = MODES[mode]
            i = ids[ring_state["i"] % len(ids)]
            ring_state["i"] += 1
            return i

        col = lambda a, n=1: COLS[:, a:a + n]

        sch.add("sp", lambda e: [e.dma_start(out=GB, in_=gains_d[0].partition_broadcast(128))],
                writes=["YC"], dma_key="gb")
        sch.add("sp", lambda e: [e.dma_start(out=XRES[:, 0, :], in_=x_d[0:128, :])], writes=xr(0), dma_key="x0")
        sch.add("sp", lambda e: [e.dma_start(out=COLS[:], in_=cols_d)], writes=["COLS"], dma_key="cols")
        WS32 = TMP[:, 0:4, :].rearrange("p a b -> p (a b)")[:, 0:NHA * 128].rearrange("p (h s) -> p h s", s=128) \
            if NHA * 128 <= 4 * SW else None
        assert WS32 is not None
        BSB = ZC[:].rearrange("p a b -> p (a b)")[:, 0:NHA * 128].rearrange("p (h t) -> p h t", t=128)
        assert NHA * 128 <= SC * (T + 2)
        sch.add("sp", lambda e: [e.dma_start(out=WS32, in_=wsp_d.rearrange("h t s -> t h s"))],
                writes=["T0", "T1", "T2", "T3"], dma_key="wsp")
        sch.add("sp", lambda e: [e.dma_start(out=BSB, in_=bsp_d[0].partition_broadcast(128).rearrange("p (h t) -> p h t", t=128))],
                writes=["ZC"], dma_key="bsp")
        sch.add("pool", lambda e: e.memset(IDF, 0.0), writes=["IDF"])
        sch.add("pool", lambda e: e.affine_select(out=IDF, in_=IDF, pattern=[[-1, 128]],
                                                  compare_op=ALU.not_equal, fill=1.0, base=0,
                                                  channel_multiplier=1), reads=["IDF"], writes=["IDF"])
        sch.add("dve", lambda e: e.tensor_copy(out=IDENT[:], in_=IDF), reads=["IDF"], writes=["IDENT"])
        sch.add("dve", lambda e: e.memset(ONESB[:], 1.0), writes=["ONESB"])
        sch.add("dve", lambda e: e.memset(ZSAVE[:], 0.0), writes=["ZSAVE"])
        sch.add("dve", lambda e: e.memset(EPSC[:], EPS), writes=["EPSC"])
        sch.add("pool", lambda e: e.affine_select(out=WS32, in_=WS32, pattern=[[0, NHA], [-1, 128]],
                                                  compare_op=ALU.is_ge, fill=0.0, base=0,
                                                  channel_multiplier=1),
                reads=["T0", "T1", "T2", "T3"], writes=["T0", "T1", "T2", "T3"])
        WSB = TMP[:, 4:6, :].rearrange("p a b -> p (a b)").bitcast(BF16)[:, 0:NHA * 128].rearrange("p (h s) -> p h s", s=128)
        sch.add("dve", lambda e: e.tensor_copy(out=WSB, in_=WS32), reads=["T0", "T1", "T2", "T3"], writes=["T4", "T5"])
        for h0 in range(0, NHA, 8):
            hs = list(range(h0, min(h0 + 8, NHA)))
            def f(e, hs=hs):
                r = None
                for j, h in enumerate(hs):
                    r = e.transpose(out=bank_bf(6)[:, j, :], in_=WSB[:, h, :], identity=IDENT[:])
                return r
            sch.add("pe", f, reads=["T4", "T5", "IDENT"], writes=["P6"])
            sch.add("act", lambda e, hs=hs: e.copy(out=WST[:, hs[0]:hs[-1] + 1, :], in_=bank_bf(6)[:, 0:len(hs), :]),
                    reads=["P6"], writes=["WST"])
        for h0 in range(0, NHA, 4):
            hs = list(range(h0, min(h0 + 4, NHA)))
            def f(e, hs=hs):
                r = None
                for j, h in enumerate(hs):
                    r = e.matmul(out=bank(7)[:, j * 128:(j + 1) * 128], lhsT=ONESB[:], rhs=WST[:, h, :],
                                 start=True, stop=True)
                return r
            sch.add("pe", f, reads=["WST", "ONESB"], writes=["P7"])
            for j, h in enumerate(hs):
                sch.add("dve", lambda e, j=j, h=h: e.scalar_tensor_tensor(
                    out=CB[:, h, :], in0=bank(7)[:, j * 128:(j + 1) * 128], scalar=col(c.c_lnvb + h),
                    in1=BSB[:, h, :], op0=ALU.mult, op1=ALU.add),
                    reads=["P7", "ZC", "COLS"], writes=["CB"])

        stc = {"i": 0}

        def stcol(n=1):
            i = stc["i"]
            if i + n > 320:
                i = 0
            stc["i"] = i + n
            return i

        def rsqrt_ops(dst, src, scale, rd, wr):
            sch.add("act", lambda e: e.activation(out=dst, in_=src, func=AF.Sqrt, bias=EPSC[:, 0:1], scale=scale),
                    reads=list(rd) + ["EPSC"], writes=[wr])
            sch.add("dve", lambda e: e.reciprocal(out=dst, in_=dst), reads=[wr], writes=[wr])

        def emit_norm(ti, gain_row, phase, load_gain=True):
            if load_gain:
                sch.add("sp", lambda e: [e.dma_start(out=GB, in_=gains_d[gain_row].partition_broadcast(128))],
                        writes=["YC"], dma_key="gb")
            JUNK = VLNf[:, 0:D]
            info = {}

            def s1(tc):
                ci = stcol(2)
                ss, rr = ST[:, ci:ci + 1], ST[:, ci + 1:ci + 2]
                sres = "st%d" % ci
                info[tc] = (rr, sres)
                sch.add("act", lambda e, tc=tc, ss=ss: e.activation(
                    out=JUNK, in_=XRES[:, tc, :], func=AF.Square, accum_out=ss),
                    reads=xr(tc), writes=VLN_RES + [sres])
                rsqrt_ops(rr, ss, 1.0 / D, [sres], sres + "r")

            def s2(tc):
                k = tc % 2
                rr, sres = info[tc]
                sch.add("dve", lambda e, tc=tc, k=k, rr=rr: e.scalar_tensor_tensor(
                    out=HN[k], in0=XRES[:, tc, :], scalar=rr, in1=GB, op0=ALU.mult, op1=ALU.mult),
                    reads=xr(tc) + [sres + "r", "YC"], writes=[HNRES[k]])

            def s3(tc):
                k = tc % 2
                for q in range(NPQ):
                    pb = 6 + (q % 2)
                    def f(e, q=q, k=k, pb=pb):
                        r = None
                        for j in range(KP):
                            dc = q * KP + j
                            r = e.transpose(out=bank_bf(pb)[:, j, :], in_=HN[k][:, dc * 128:(dc + 1) * 128],
                                            identity=IDENT[:])
                        return r
                    sch.add("pe", f, reads=[HNRES[k], "IDENT"], writes=["P%d" % pb])
                    eng = "act" if (q % 2 == 0) else "dve"
                    def g(e, q=q, tc=tc, pb=pb, eng=eng):
                        o = HT[:, q * KP:(q + 1) * KP, tc * 128:(tc + 1) * 128]
                        i = bank_bf(pb)[:, 0:KP, :]
                        return e.copy(out=o, in_=i) if eng == "act" else e.tensor_copy(out=o, in_=i)
                    sch.add(eng, g, reads=["P%d" % pb], writes=["HT%d_%d" % (q, tc)])

            s1(0)
            for tc in range(TC):
                if tc + 1 < TC:
                    s1(tc + 1)
                s2(tc)
                if tc >= 1:
                    s3(tc - 1)
            s3(TC - 1)

        def ht_res(q, tc=None):
            if tc is None:
                return ["HT%d_%d" % (q, t) for t in range(TC)]
            return ["HT%d_%d" % (q, tc)]

        def weight_dma(slot_i, runs):
            def f(e):
                return [e.dma_start(out=o, in_=i) for (o, i) in runs]
            sch.add("pool", f, writes=[slot_res[slot_i]], dma_key="w%d" % slot_i, n_dma=len(runs))

        def slot3(slot_i, nk, width):
            return slots[slot_i][:, 0:nk * width].rearrange("p (k w) -> p k w", w=width)

        def proj_fm(src_v, col_runs, nch, mode, extra_after_piece=None):
            width = sum(n for (_, n) in col_runs)
            for q in range(NPQ):
                si = next_slot(mode)
                s3 = slot3(si, KP, width)
                runs, off = [], 0
                for (c0, n) in col_runs:
                    runs.append((s3[:, :, off:off + n], src_v[:, q * KP:(q + 1) * KP, c0:c0 + n]))
                    off += n
                weight_dma(si, runs)
                for ci in range(nch):
                    def f(e, q=q, ci=ci, s3=s3):
                        r = None
                        for j in range(KP):
                            r = e.matmul(out=bank(ci)[:, 0:T], lhsT=s3[:, j, ci * 128:(ci + 1) * 128],
                                         rhs=HT[:, q * KP + j, :],
                                         start=(q == 0 and j == 0), stop=(q == NPQ - 1 and j == KP - 1))
                        return r
                    sch.add("pe", f, reads=[slot_res[si]] + ht_res(q), writes=["P%d" % ci])
                if extra_after_piece is not None:
                    extra_after_piece(q, s3, si)

        for ti in range(c.NTILE):
            r0 = ti * T
            stc["i"] = 0
            for tc in range(TC):
                if ti == 0 and tc == 0:
                    continue
                sch.add("sp", lambda e, tc=tc, r0=r0: [e.dma_start(out=XRES[:, tc, :],
                                                                  in_=x_d[r0 + tc * 128:r0 + (tc + 1) * 128, :])],
                        writes=xr(tc), dma_key="x%d" % tc)
            emit_norm(ti, 0, "mix", load_gain=(ti != 0))

            if ti == 0:
                sch.add("sp", lambda e: [e.dma_start(out=XH[:], in_=xh_d)], writes=["XH"], dma_key="xh")
                sch.add("act", lambda e: e.activation(out=XHS[:], in_=XH[:], func=AF.Square), reads=["XH"], writes=["XHS"])
                hci = stcol(4)
                hs_, hr_ = ST[:, hci:hci + 2], ST[:, hci + 2:hci + 4]
                sch.add("dve", lambda e: e.tensor_reduce(out=hs_, in_=XHS[:].rearrange("p dc j -> p j dc"),
                                                         axis=AX.X, op=ALU.add), reads=["XHS"], writes=["sth"])
                sch.add("dve", lambda e: e.memset(ONESF, 1.0), writes=["ZC"])
                sch.add("pe", lambda e: e.matmul(out=bank(6)[:, 0:2], lhsT=ONESF, rhs=hs_, start=True, stop=True),
                        reads=["sth", "ZC"], writes=["P6"])
                rsqrt_ops(hr_, bank(6)[:, 0:2], 1.0 / D, ["P6"], "sthr")
                sch.add("dve", lambda e: e.tensor_tensor(
                    out=XHS[:], in0=XH[:], in1=col(c.c_gmix, DC).unsqueeze(2).broadcast_to([128, DC, 2]), op=ALU.mult),
                    reads=["XH", "COLS", "XHS"], writes=["XHS"])
                sch.add("dve", lambda e: e.tensor_tensor(
                    out=HTH[:], in0=XHS[:], in1=hr_.unsqueeze(1).broadcast_to([128, DC, 2]), op=ALU.mult),
                    reads=["XHS", "sthr"], writes=["HTH"])

            pend = []

            def flush_pend(n_keep=0):
                while len(pend) > n_keep:
                    pend.pop(0)()

            def stats_op(k2, bankno, colbase, first):
                def f(e):
                    r = None
                    for tc in range(TC):
                        r = e.matmul(out=bank(bankno)[:, colbase + tc:colbase + tc + 1],
                                     lhsT=YSQ[:, k2, tc * 128:(tc + 1) * 128], rhs=ONESB[:, 0:1],
                                     start=(first and tc == 0), stop=False, skip_group_check=True)
                    return r
                sch.add("pe", f, reads=["YSQ%d" % k2, "ONESB"], writes=["P%d" % bankno])

            hook = {"f": None}
            for vj in range(NHA // SC):
                c0 = c.DA + vj * SW
                for q in range(NPQ):
                    si = next_slot("early")
                    s3 = slot3(si, KP, SW)
                    weight_dma(si, [(s3, win_v[:, q * KP:(q + 1) * KP, c0:c0 + SW])])
                    for tc in range(TC):
                        def f(e, q=q, tc=tc, s3=s3):
                            r = None
                            for j in range(KP):
                                r = e.matmul(out=bank(tc)[:, 0:SW], lhsT=HT[:, q * KP + j, tc * 128:(tc + 1) * 128],
                                             rhs=s3[:, j, :], start=(q == 0 and j == 0),
                                             stop=(q == NPQ - 1 and j == KP - 1))
                            return r
                        sch.add("pe", f, reads=[slot_res[si]] + ht_res(q, tc), writes=["P%d" % tc])
                nst = TC * SC
                ci0 = stcol(4 * nst)
                S1, S2 = ST[:, ci0:ci0 + nst], ST[:, ci0 + nst:ci0 + 2 * nst]
                MQ, RS = ST[:, ci0 + 2 * nst:ci0 + 3 * nst], ST[:, ci0 + 3 * nst:ci0 + 4 * nst]
                rn = "sv%d" % ci0
                for tc in range(TC):
                    sch.add("act", lambda e, tc=tc: e.activation(out=TMP[:, tc, :], in_=bank(tc)[:, 0:SW], func=AF.Gelu_apprx_tanh),
                            reads=["P%d" % tc], writes=["T%d" % tc])
                for tc in range(TC):
                    k = tc % 2
                    vg3 = TMP[:, tc, :].rearrange("p (h d) -> p h d", d=128)
                    vsq3 = TMP[:, 4 + k, :].rearrange("p (h d) -> p h d", d=128)
                    sch.add("act", lambda e, tc=tc, k=k: e.activation(out=TMP[:, 4 + k, :], in_=TMP[:, tc, :], func=AF.Square),
                            reads=["T%d" % tc], writes=["T%d" % (4 + k)])
                    sch.add("dve", lambda e, tc=tc, vg3=vg3, S1=S1: e.tensor_reduce(out=S1[:, tc * SC:(tc + 1) * SC], in_=vg3, axis=AX.X, op=ALU.add),
                            reads=["T%d" % tc], writes=[rn + "a"])
                    sch.add("dve", lambda e, tc=tc, vsq3=vsq3, S2=S2: e.tensor_reduce(out=S2[:, tc * SC:(tc + 1) * SC], in_=vsq3, axis=AX.X, op=ALU.add),
                            reads=["T%d" % (4 + k)], writes=[rn + "b"])
                sch.add("dve", lambda e, S1=S1: e.tensor_scalar(out=S1, in0=S1, scalar1=1.0 / 128, scalar2=None, op0=ALU.mult),
                        reads=[rn + "a"], writes=[rn + "a"])
                sch.add("dve", lambda e, S1=S1, MQ=MQ: e.tensor_tensor(out=MQ, in0=S1, in1=S1, op=ALU.mult),
                        reads=[rn + "a"], writes=[rn + "c"])
                sch.add("dve", lambda e, RS=RS, S2=S2, MQ=MQ: e.scalar_tensor_tensor(
                    out=RS, in0=S2, scalar=1.0 / 128, in1=MQ, op0=ALU.mult, op1=ALU.subtract),
                    reads=[rn + "b", rn + "c"], writes=[rn + "d"])
                rsqrt_ops(RS, RS, 1.0, [rn + "d"], rn + "d")
                for tc in range(TC):
                    vg3 = TMP[:, tc, :].rearrange("p (h d) -> p h d", d=128)
                    sch.add("dve", lambda e, vg3=vg3, tc=tc, S1=S1: e.tensor_tensor(
                        out=vg3, in0=vg3, in1=S1[:, tc * SC:(tc + 1) * SC].unsqueeze(2).broadcast_to([128, SC, 128]), op=ALU.subtract),
                        reads=["T%d" % tc, rn + "a"], writes=["T%d" % tc])
                    sch.add("dve", lambda e, vg3=vg3, tc=tc, vj=vj, RS=RS: e.tensor_tensor(
                        out=VLN[:, tc, vj * SW:(vj + 1) * SW].rearrange("p (h d) -> p h d", d=128), in0=vg3,
                        in1=RS[:, tc * SC:(tc + 1) * SC].unsqueeze(2).broadcast_to([128, SC, 128]), op=ALU.mult),
                        reads=["T%d" % tc, rn + "d"], writes=VLN_RES)

            def after_piece1(q, s3, si):
                if q == min(1, NPQ - 1):
                    flush_pend(0)

            for uj in range(NHA // SC):
                c0 = uj * SW
                proj_fm(win_v, [(c0, SW)], SC, "early", after_piece1)
                for ci in range(SC):
                    sch.add("act", lambda e, ci=ci: e.activation(out=TMP[:, ci, :][:, 0:T], in_=bank(ci)[:, 0:T], func=AF.Gelu_apprx_tanh),
                            reads=["P%d" % ci], writes=["T%d" % ci])
                for ci in range(SC):
                    h = uj * SC + ci
                    k = h % 2
                    gu, zz = TMP[:, ci, :][:, 0:T], TMP[:, 4 + k, :][:, 0:T]
                    sb_ = 4 + k
                    def f(e, h=h, sb_=sb_):
                        r = None
                        for tc in range(TC):
                            r = e.matmul(out=bank(sb_)[:, tc * 128:(tc + 1) * 128], lhsT=VLN[:, tc, h * 128:(h + 1) * 128],
                                         rhs=WST[:, h, :], start=True, stop=True)
                        return r
                    sch.add("pe", f, reads=VLN_RES + ["WST"], writes=["P%d" % sb_])
                    zz3 = zz.rearrange("p (tc t) -> p tc t", t=128)
                    sch.add("dve", lambda e, h=h, sb_=sb_, zz3=zz3: e.scalar_tensor_tensor(
                        out=zz3, in0=bank(sb_)[:, 0:T].rearrange("p (tc t) -> p tc t", t=128), scalar=col(c.c_lnvg + h),
                        in1=CB[:, h, :].unsqueeze(1).broadcast_to([128, TC, 128]), op0=ALU.mult, op1=ALU.add),
                        reads=["P%d" % sb_, "CB", "COLS"], writes=["T%d" % (4 + k)])
                    sch.add("dve", lambda e, gu=gu, zz=zz: e.tensor_tensor(out=zz, in0=gu, in1=zz, op=ALU.mult),
                            reads=["T%d" % ci, "T%d" % (4 + k)], writes=["T%d" % (4 + k)])
                    sch.add("act", lambda e, h=h, zz=zz: e.mul(out=YT[:, h, :], in_=zz, mul=col(c.c_ga + h)),
                            reads=["T%d" % (4 + k), "COLS"], writes=[yres(h)])
                    sch.add("act", lambda e, zz=zz, ci=ci: e.activation(out=YSQ[:, ci, :], in_=zz, func=AF.Square),
                            reads=["T%d" % (4 + k)], writes=["YSQ%d" % ci])
                    pend.append(lambda ci=ci, first=(h == 0): stats_op(ci, 7, 0, first))

            hb = SC // 2
            nbj = NCB // SC
            nb_early = nbj // 2
            first_cx = True
            for bj in range(nbj):
                bmode = "early" if bj < nb_early else "late"
                if bj == nb_early:
                    for tc in range(TC):
                        sch.add("sp", lambda e, tc=tc, r0=r0: [e.dma_start(
                            out=XRES[:, tc, :], in_=x_d[r0 + tc * 128:r0 + (tc + 1) * 128, :])],
                            writes=xr(tc), dma_key="x%d" % tc)
                for half in range(2):
                    cxk = bj * 2 + half
                    cC = 2 * c.DA + c.DB + cxk * hb * 128
                    cX = 2 * c.DA + 2 * c.DB + cxk * hb * 128

                    def extra(q, s3, si, cxk=cxk):
                        after_piece1(q, s3, si)
                        if ti != 0:
                            return
                        def f(e):
                            r = None
                            for ci in range(SC):
                                for j in range(KP):
                                    r = e.matmul(out=bank(6)[:, 2 * ci:2 * ci + 2], lhsT=s3[:, j, ci * 128:(ci + 1) * 128],
                                                 rhs=HTH[:, q * KP + j, :],
                                                 start=(q == 0 and ci == 0 and j == 0), stop=False,
                                                 skip_group_check=True)
                            return r
                        sch.add("pe", f, reads=[slot_res[si], "HTH"], writes=["P6"])
                    proj_fm(win_v, [(cC, hb * 128), (cX, hb * 128)], SC, bmode, extra)
                    if first_cx:
                        first_cx = False
                        cA = stcol(TC)
                        RA = ST[:, cA:cA + TC]
                        rsqrt_ops(RA, bank(7)[:, 0:TC], 1.0 / c.DA, ["P7"], "RA")
                    for i in range(hb):
                        sch.add("act", lambda e, i=i: e.copy(out=TMP[:, i, :][:, 0:T], in_=bank(i)[:, 0:T]),
                                reads=["P%d" % i], writes=["T%d" % i])
                    if ti == 0:
                        sch.add("act", lambda e: e.copy(out=CH[:, 0:hb, :], in_=bank(6)[:, 0:2 * hb].rearrange("p (a b) -> p a b", b=2)),
                                reads=["P6"], writes=["CH"])
                    for i in range(hb):
                        zi = half * hb + i
                        cglob = cxk * hb + i
                        ct = TMP[:, i, :][:, 0:T]
                        sch.add("dve", lambda e, i=i, zi=zi, ct=ct: e.tensor_tensor(
                            out=ZC[:, zi, 2:T + 2], in0=bank(hb + i)[:, 0:T], in1=ct, op=ALU.mult),
                            reads=["P%d" % (hb + i), "T%d" % i], writes=["ZC"])
                        if ti == 0:
                            sch.add("dve", lambda e, i=i, zi=zi: e.tensor_tensor(
                                out=ZC[:, zi, 0:2], in0=bank(6)[:, 2 * (hb + i):2 * (hb + i) + 2], in1=CH[:, i, :], op=ALU.mult),
                                reads=["P6", "CH"], writes=["ZC"])
                        else:
                            sch.add("dve", lambda e, zi=zi, cglob=cglob: e.tensor_copy(out=ZC[:, zi, 0:2], in_=ZSAVE[:, cglob, :]),
                                    reads=["ZSAVE"], writes=["ZC"])
                        sch.add("dve", lambda e, zi=zi, cglob=cglob: e.tensor_copy(out=ZSAVE[:, cglob, :], in_=ZC[:, zi, T:T + 2]),
                                reads=["ZC", "ZSAVE"], writes=["ZSAVE"])
                for ci in range(SC):
                    cb = bj * SC + ci
                    t1 = TMP[:, 2 + ci, :][:, 0:T]
                    w = lambda kk, cb=cb: col(c.c_cw + kk * NCB + cb)
                    sch.add("act", lambda e, ci=ci, t1=t1, w=w: e.mul(out=t1, in_=ZC[:, ci, 0:T], mul=w(0)),
                            reads=["ZC", "COLS"], writes=["T%d" % (2 + ci)])
                    sch.add("dve", lambda e, ci=ci, t1=t1, w=w: e.scalar_tensor_tensor(
                        out=t1, in0=ZC[:, ci, 1:T + 1], scalar=w(1), in1=t1, op0=ALU.mult, op1=ALU.add),
                        reads=["ZC", "COLS", "T%d" % (2 + ci)], writes=["T%d" % (2 + ci)])
                    sch.add("dve", lambda e, ci=ci, t1=t1, w=w: e.scalar_tensor_tensor(
                        out=t1, in0=ZC[:, ci, 2:T + 2], scalar=w(2), in1=t1, op0=ALU.mult, op1=ALU.add),
                        reads=["ZC", "COLS", "T%d" % (2 + ci)], writes=["T%d" % (2 + ci)])
                cB = 2 * c.DA + bj * SW
                proj_fm(win_v, [(cB, SW)], SC, bmode, after_piece1)
                for ci in range(SC):
                    t1 = TMP[:, 2 + ci, :][:, 0:T]
                    sch.add("dve", lambda e, ci=ci, t1=t1: e.tensor_tensor(out=t1, in0=bank(ci)[:, 0:T], in1=t1, op=ALU.mult),
                            reads=["P%d" % ci, "T%d" % (2 + ci)], writes=["T%d" % (2 + ci)])
                for ci in range(SC):
                    cb = bj * SC + ci
                    t1 = TMP[:, 2 + ci, :][:, 0:T]
                    sch.add("act", lambda e, cb=cb, t1=t1: e.mul(out=YT[:, NHA + cb, :], in_=t1, mul=col(c.c_gb + cb)),
                            reads=["T%d" % (2 + ci), "COLS"], writes=[yres(NHA + cb)])
                    sch.add("act", lambda e, t1=t1, ci=ci: e.activation(out=YSQ[:, ci, :], in_=t1, func=AF.Square),
                            reads=["T%d" % (2 + ci)], writes=["YSQ%d" % ci])
                    pend.append(lambda ci=ci, first=(cb == 0): stats_op(ci, 4, 0, first))

            cBc = stcol(TC)
            RB = ST[:, cBc:cBc + TC]
            rb_done = False
            for ds in range(c.DS):
                d0 = ds * c.DSW
                for part, (cc0, ncc, RR, rname) in enumerate(((0, NHA, RA, "RA"), (NHA, NCB, RB, "RB"))):
                    npq = ncc // KP
                    for q in range(npq):
                        si = next_slot("late")
                        s3 = slot3(si, KP, c.DSW)
                        weight_dma(si, [(s3, wout_v[:, cc0 + q * KP:cc0 + (q + 1) * KP, d0:d0 + c.DSW])])
                        for tc in range(TC):
                            def f(e, q=q, tc=tc, s3=s3, cc0=cc0, npq=npq):
                                r = None
                                for j in range(KP):
                                    r = e.matmul(out=bank(tc)[:, 0:c.DSW], lhsT=YT[:, cc0 + q * KP + j, tc * 128:(tc + 1) * 128],
                                                 rhs=s3[:, j, :], start=(q == 0 and j == 0),
                                                 stop=(q == npq - 1 and j == KP - 1))
                                return r
                            yr = sorted(set(yres(cc0 + q * KP + j) for j in range(KP)))
                            sch.add("pe", f, reads=[slot_res[si]] + yr, writes=["P%d" % tc])
                        if not rb_done:
                            rb_done = True
                            flush_pend(0)
                            rsqrt_ops(RB, bank(4)[:, 0:TC], 1.0 / c.DB, ["P4"], "RB")
                    for tc in range(TC):
                        sch.add("dve", lambda e, tc=tc, RR=RR, d0=d0: e.scalar_tensor_tensor(
                            out=XRES[:, tc, d0:d0 + c.DSW], in0=bank(tc)[:, 0:c.DSW], scalar=RR[:, tc:tc + 1],
                            in1=XRES[:, tc, d0:d0 + c.DSW], op0=ALU.mult, op1=ALU.add),
                            reads=["P%d" % tc, rname] + xr(tc), writes=xr(tc))

            emit_norm(ti, 1, "ffn")

            NG = (FC + G - 1) // G
            dbank = {"i": 0}
            deferred_down = []

            def emit_down(gj, nchg, k):
                for ds in range(c.DS):
                    d0 = ds * c.DSW
                    si = next_slot("late")
                    s3 = slot3(si, nchg, c.DSW)
                    weight_dma(si, [(s3, wd_v[:, gj * G:gj * G + nchg, d0:d0 + c.DSW])])
                    for tc in range(TC):
                        b = 4 + (dbank["i"] % 4); dbank["i"] += 1
                        def f(e, tc=tc, s3=s3, b=b, nchg=nchg, k=k):
                            r = None
                            for gi in range(nchg):
                                r = e.matmul(out=bank(b)[:, 0:c.DSW], lhsT=AT[k][:, gi, tc * 128:(tc + 1) * 128],
                                             rhs=s3[:, gi, :], start=(gi == 0), stop=(gi == nchg - 1))
                            return r
                        sch.add("pe", f, reads=[slot_res[si], ATRES[k]], writes=["P%d" % b])
                        sch.add("dve", lambda e, tc=tc, b=b, d0=d0: e.tensor_tensor(
                            out=XRES[:, tc, d0:d0 + c.DSW], in0=bank(b)[:, 0:c.DSW], in1=XRES[:, tc, d0:d0 + c.DSW], op=ALU.add),
                            reads=["P%d" % b] + xr(tc), writes=xr(tc))

            for gj in range(NG):
                f0 = gj * G
                nchg = min(G, FC - f0)
                k = gj % 2
                for s0 in range(0, nchg, SC):
                    nch = min(SC, nchg - s0)
                    cF = (f0 + s0) * 128
                    proj_fm(wg_v, [(cF, nch * 128)], nch, "late")
                    for ci in range(nch):
                        sch.add("act", lambda e, ci=ci: e.activation(out=SG[:, ci, :], in_=bank(ci)[:, 0:T], func=AF.Silu),
                                reads=["P%d" % ci], writes=["SG%d" % ci])
                    proj_fm(wu_v, [(cF, nch * 128)], nch, "late")
                    for ci in range(nch):
                        sch.add("dve", lambda e, ci=ci, k=k, s0=s0: e.tensor_tensor(
                            out=AT[k][:, s0 + ci, :], in0=bank(ci)[:, 0:T], in1=SG[:, ci, :], op=ALU.mult),
                            reads=["P%d" % ci, "SG%d" % ci], writes=[ATRES[k]])
                    if s0 == 0 and deferred_down:
                        deferred_down.pop(0)()
                deferred_down.append(lambda gj=gj, nchg=nchg, k=k: emit_down(gj, nchg, k))
                if gj == 0:
                    pass
            while deferred_down:
                deferred_down.pop(0)()

            sch.add("sp", lambda e: [e.dma_start(out=GB, in_=gains_d[2].partition_broadcast(128))],
                    writes=["YC"], dma_key="gb")
            finfo = {}
            n_stg = max(1, min(2, (DC * T * 2) // (D * 4)))
            stg_elems = (DC * T) // n_stg
            STG = [HTF[:, i * (stg_elems // 2):i * (stg_elems // 2) + D] for i in range(n_stg)]
            qs_per = NPQ // n_stg
            STGRES = [[r for q in range(i * qs_per, (i + 1) * qs_per if i < n_stg - 1 else NPQ) for r in ht_res(q)]
                      for i in range(n_stg)]
            FJUNK = TMP[:].rearrange("p a b -> p (a b)").bitcast(BF16)[:, 0:D]
            assert 6 * SW * 2 >= D
            tmpres = ["T%d" % i for i in range(6)]
            for tc in range(TC):
                ci = stcol(2)
                ss, rr = ST[:, ci:ci + 1], ST[:, ci + 1:ci + 2]
                sres = "sf%d" % ci
                finfo[tc] = (rr, sres)
                sch.add("act", lambda e, tc=tc, ss=ss: e.activation(out=FJUNK, in_=XRES[:, tc, :], func=AF.Square,
                                                                    accum_out=ss),
                        reads=xr(tc), writes=tmpres + [sres])
                rsqrt_ops(rr, ss, 1.0 / D, [sres], sres + "r")
            for tc in range(TC):
                rr, sres = finfo[tc]
                k = tc % n_stg
                sch.add("dve", lambda e, tc=tc, rr=rr, k=k: e.scalar_tensor_tensor(
                    out=STG[k], in0=XRES[:, tc, :], scalar=rr, in1=GB, op0=ALU.mult, op1=ALU.mult),
                    reads=xr(tc) + [sres + "r", "YC"], writes=STGRES[k])
                sch.add("sp", lambda e, tc=tc, r0=r0, k=k: [e.dma_start(out=out_d[r0 + tc * 128:r0 + (tc + 1) * 128, :], in_=STG[k])],
                        reads=STGRES[k], dma_key="o%d" % tc)

        sems = {}
        for n in sch.sem_names():
            sems[n] = es.enter_context(nc.semaphore(n.replace(":", "_")))
        with nc.allow_low_precision("bf16 matmul operands, fp32 PSUM accumulation"):
            with nc.Block() as block:
                @block.tensor
                def _(e):
                    sch.emit("pe", e, sems)

                @block.scalar
                def _(e):
                    sch.emit("act", e, sems)

                @block.vector
                def _(e):
                    sch.emit("dve", e, sems)

                @block.gpsimd
                def _(e):
                    sch.emit("pool", e, sems)

                @block.sync
                def _(e):
                    sch.emit("sp", e, sems)
                    for tc in range(TC):
                        e.wait_ge(sems["d:o%d" % tc], 16 * sch.dcnt["o%d" % tc])
    return nc


def make_in_maps(cfg, x2d, p):
    c = cfg
    rows = c.NTILE * c.T
    cols = np.zeros((128, c.NCOLS), np.float32)
    cols[:, c.c_lnvg:c.c_lnvg + c.NHA] = p["ln_v_g"].reshape(c.NHA, 128).T
    cols[:, c.c_lnvb:c.c_lnvb + c.NHA] = p["ln_v_b"].reshape(c.NHA, 128).T
    cols[:, c.c_ga:c.c_ga + c.NHA] = p["out_norm_a_g"].reshape(c.NHA, 128).T
    cols[:, c.c_gb:c.c_gb + c.NCB] = p["out_norm_b_g"].reshape(c.NCB, 128).T
    cols[:, c.c_cw:c.c_cw + 3 * c.NCB] = p["conv_w"].reshape(3, c.NCB, 128).transpose(2, 0, 1).reshape(128, 3 * c.NCB)
    cols[:, c.c_gmix:c.c_gmix + c.DC] = p["mix_norm_g"].reshape(c.DC, 128).T
    gains = np.ascontiguousarray(np.stack([p["mix_norm_g"], p["ffn_norm_g"], p["final_norm_g"]]).astype(np.float32))
    shared = {
        "cols": cols, "gains": gains,
        "w_spatial": np.ascontiguousarray(p["w_spatial"], dtype=np.float32),
        "b_spatial": np.ascontiguousarray(p["b_spatial"].reshape(1, -1), dtype=np.float32),
        "w_in": np.ascontiguousarray(p["w_in"], dtype=np.float32),
        "w_out": np.ascontiguousarray(p["w_out"], dtype=np.float32),
        "w_gate": np.ascontiguousarray(p["w_gate"], dtype=np.float32),
        "w_up": np.ascontiguousarray(p["w_up"], dtype=np.float32),
        "w_down": np.ascontiguousarray(p["w_down"], dtype=np.float32),
    }
    in_maps = []
    for i in range(c.NCORES):
        xs = np.ascontiguousarray(x2d[i * rows:(i + 1) * rows])
        halo = np.zeros((2, c.D), np.float32)
        if i > 0:
            halo[:] = x2d[i * rows - 2:i * rows]
        xh = np.ascontiguousarray(halo.reshape(2, c.DC, 128).transpose(2, 1, 0))
        m = dict(shared)
        m["x"] = xs
        m["xh"] = xh
        in_maps.append(m)
    return in_maps


_PROGRAM_CACHE = {}


def kernel(x, mix_norm_g, w_in, ln_v_g, ln_v_b, w_spatial, b_spatial, conv_w,
           out_norm_a_g, out_norm_b_g, w_out, ffn_norm_g, w_gate, w_up, w_down, final_norm_g):
    cfg = Cfg()
    x = np.asarray(x, dtype=np.float32)
    B, S, D = x.shape
    assert D == cfg.D and B * S == cfg.NCORES * cfg.NTILE * cfg.T
    p = {
        "mix_norm_g": np.asarray(mix_norm_g)[0], "w_in": np.asarray(w_in)[0],
        "ln_v_g": np.asarray(ln_v_g)[0], "ln_v_b": np.asarray(ln_v_b)[0],
        "w_spatial": np.asarray(w_spatial)[0], "b_spatial": np.asarray(b_spatial)[0],
        "conv_w": np.asarray(conv_w)[0], "out_norm_a_g": np.asarray(out_norm_a_g)[0],
        "out_norm_b_g": np.asarray(out_norm_b_g)[0], "w_out": np.asarray(w_out)[0],
        "ffn_norm_g": np.asarray(ffn_norm_g)[0], "w_gate": np.asarray(w_gate)[0],
        "w_up": np.asarray(w_up)[0], "w_down": np.asarray(w_down)[0],
        "final_norm_g": np.asarray(final_norm_g),
    }
    in_maps = make_in_maps(cfg, x.reshape(B * S, D), p)
    if "nc" not in _PROGRAM_CACHE:
        _PROGRAM_CACHE["nc"] = build_program(cfg)
    nc = _PROGRAM_CACHE["nc"]
    res = run_bass_kernel_spmd(nc, in_maps, core_ids=list(range(cfg.NCORES)))
    out = np.concatenate([np.asarray(r["out"]) for r in res.results], axis=0)
    return out.reshape(B, S, D).astype(np.float32)
```

```python
import numpy as np
from contextlib import ExitStack

import concourse.bass as bass
import concourse.mybir as mybir
from concourse.bass_utils import run_bass_kernel_spmd

F32 = mybir.dt.float32
BF16 = mybir.dt.bfloat16
AF = mybir.ActivationFunctionType
ALU = mybir.AluOpType
AX = mybir.AxisListType
EPS = 1e-6


class Cfg:
    def __init__(self, D=4096, NHA=16, NCB=16, F=11008, T=512, NTILE=2, SC=4, KP=8, R=3,
                 NCORES=8):
        self.D, self.NHA, self.NCB, self.F, self.T, self.NTILE = D, NHA, NCB, F, T, NTILE
        self.SC, self.KP, self.R, self.NCORES = SC, KP, R, NCORES
        self.DC = D // 128
        self.TC = T // 128
        self.FC = F // 128
        self.DA, self.DB = NHA * 128, NCB * 128
        self.MIXC = NHA + NCB
        self.MIX = self.MIXC * 128
        self.INC = 2 * self.DA + 3 * self.DB
        self.SW = SC * 128
        self.NPQ = self.DC // KP
        self.G = self.MIXC // 4
        self.DS = D // 512 if D >= 512 else 1
        self.DSW = min(512, D)
        assert self.MIXC * T == 4 * D, "YT aliasing (hn0|hn1|GB) needs MIXC*T == 4*D"
        assert self.DC % KP == 0 and NHA % KP == 0 and NCB % KP == 0
        assert NHA % SC == 0 and NCB % SC == 0 and SC % 2 == 0
        assert self.G <= KP and self.G % SC == 0
        assert self.TC * self.DA == 2 * KP * self.SW or True
        o = 0
        self.c_lnvg = o; o += NHA
        self.c_lnvb = o; o += NHA
        self.c_ga = o; o += NHA
        self.c_gb = o; o += NCB
        self.c_cw = o; o += 3 * NCB
        self.c_gmix = o; o += self.DC
        self.NCOLS = o


class _Op:
    __slots__ = ("eng", "emit", "deps", "raw", "is_dma", "key", "done", "n_dma")


class _FirstRecorder:
    def __init__(self, eng):
        self._eng = eng
        self.first = None

    def __getattr__(self, name):
        fn = getattr(self._eng, name)

        def wrapped(*a, **k):
            r = fn(*a, **k)
            if self.first is None:
                self.first = r
            return r
        return wrapped


class Sched:
    ENGS = ("pe", "act", "dve", "pool", "sp")

    def __init__(self):
        self.ops = []
        self.lastw = {}
        self.readers = {}
        self.cnt = {e: 0 for e in self.ENGS}
        self.dcnt = {}

    def add(self, eng, emit, reads=(), writes=(), dma_key=None, n_dma=1):
        idx = len(self.ops)
        deps, raw = set(), set()
        for r in reads:
            if r in self.lastw:
                deps.add(self.lastw[r]); raw.add(self.lastw[r])
            if r[0] == "P":
                for rd in self.readers.get(r, ()):
                    if self.ops[rd].eng != eng:
                        deps.add(rd)
        for w in writes:
            if w in self.lastw:
                deps.add(self.lastw[w]); raw.add(self.lastw[w])
            deps.update(self.readers.get(w, ()))
        op = _Op()
        op.eng, op.emit, op.is_dma, op.key, op.n_dma = eng, emit, dma_key is not None, dma_key, n_dma
        keep = set()
        for d in deps:
            dop = self.ops[d]
            if dop.is_dma:
                keep.add(d)
            elif dop.eng == eng:
                if eng != "pe" and d in raw:
                    keep.add(d)
            else:
                keep.add(d)
        op.deps = keep
        if op.is_dma:
            self.dcnt[dma_key] = self.dcnt.get(dma_key, 0) + n_dma
            op.done = ("d:" + dma_key, 16 * self.dcnt[dma_key])
        else:
            self.cnt[eng] += 1
            op.done = ("e:" + eng, self.cnt[eng])
        for r in reads:
            self.readers.setdefault(r, []).append(idx)
        for w in writes:
            self.lastw[w] = idx
            self.readers[w] = []
        self.ops.append(op)
        return idx

    def sem_names(self):
        names = ["e:" + e for e in self.ENGS]
        names += ["d:" + k for k in self.dcnt]
        return names

    def emit(self, eng_name, eng, sems, attach=False):
        known = {}
        for op in self.ops:
            if op.eng != eng_name:
                continue
            waits = {}
            for d in op.deps:
                s, v = self.ops[d].done
                if v > waits.get(s, 0):
                    waits[s] = v
            need = [(s, v) for s, v in waits.items() if known.get(s, 0) < v]
            fused = need.pop() if (attach and need and not op.is_dma) else None
            for s, v in need:
                eng.wait_ge(sems[s], v)
                known[s] = v
            if fused is not None:
                rec = _FirstRecorder(eng)
                res = op.emit(rec)
                rec.first._wait_ge(sems[fused[0]], fused[1])
                known[fused[0]] = fused[1]
            else:
                res = op.emit(eng)
            if op.is_dma:
                assert len(res) == op.n_dma
                for ins in res:
                    ins.then_inc(sems[op.done[0]], 16)
            else:
                res.then_inc(sems[op.done[0]], 1)


def build_program(cfg):
    c = cfg
    D, T, TC, DC, SC, KP, SW, NPQ = c.D, c.T, c.TC, c.DC, c.SC, c.KP, c.SW, c.NPQ
    NHA, NCB, MIXC, G, FC = c.NHA, c.NCB, c.MIXC, c.G, c.FC
    ROWS = c.NTILE * T

    nc = bass.Bass("TRN2", target_bir_lowering=False)
    x_d = nc.dram_tensor("x", [ROWS, D], F32, kind="ExternalInput").ap()
    xh_d = nc.dram_tensor("xh", [128, DC, 2], F32, kind="ExternalInput").ap()
    cols_d = nc.dram_tensor("cols", [128, c.NCOLS], F32, kind="ExternalInput").ap()
    gains_d = nc.dram_tensor("gains", [3, D], F32, kind="ExternalInput").ap()
    wsp_d = nc.dram_tensor("w_spatial", [NHA, 128, 128], F32, kind="ExternalInput").ap()
    bsp_d = nc.dram_tensor("b_spatial", [1, NHA * 128], F32, kind="ExternalInput").ap()
    win_d = nc.dram_tensor("w_in", [D, c.INC], F32, kind="ExternalInput").ap()
    wout_d = nc.dram_tensor("w_out", [c.MIX, D], F32, kind="ExternalInput").ap()
    wg_d = nc.dram_tensor("w_gate", [D, c.F], F32, kind="ExternalInput").ap()
    wu_d = nc.dram_tensor("w_up", [D, c.F], F32, kind="ExternalInput").ap()
    wd_d = nc.dram_tensor("w_down", [c.F, D], F32, kind="ExternalInput").ap()
    out_d = nc.dram_tensor("out", [ROWS, D], F32, kind="ExternalOutput").ap()

    win_v = win_d.rearrange("(dc p) c -> p dc c", p=128)
    wout_v = wout_d.rearrange("(cc p) d -> p cc d", p=128)
    wg_v = wg_d.rearrange("(dc p) f -> p dc f", p=128)
    wu_v = wu_d.rearrange("(dc p) f -> p dc f", p=128)
    wd_v = wd_d.rearrange("(fc p) d -> p fc d", p=128)

    es = ExitStack()
    with es:
        def sb(name, shape, dt):
            return es.enter_context(nc.sbuf_tensor(name, shape, dt))

        XRES = sb("XRES", [128, TC, D], F32)
        HT = sb("HT", [128, DC, T], BF16)
        YTf = sb("YT", [128, MIXC * T], BF16)
        VLNf = sb("VLN", [128, TC * c.DA], BF16)
        SLOT = KP * max(SW, c.DSW)
        RING = sb("RING", [128, c.R, SLOT], BF16)
        TMP = sb("TMP", [128, 6, SW], F32)
        ZC = sb("ZC", [128, SC, T + 2], F32)
        YSQ = sb("YSQ", [128, SC, T], BF16)
        IDENT = sb("IDENT", [128, 128], BF16)
        ONESB = sb("ONESB", [128, 128], BF16)
        WST = sb("WST", [128, NHA, 128], BF16)
        CB = sb("CB", [128, NHA, 128], F32)
        COLS = sb("COLS", [128, c.NCOLS], F32)
        ST = sb("ST", [128, 320], F32)
        XH = sb("XH", [128, DC, 2], F32)
        XHS = sb("XHS", [128, DC, 2], F32)
        HTH = sb("HTH", [128, DC, 2], BF16)
        ZSAVE = sb("ZSAVE", [128, NCB, 2], F32)
        CH = sb("CH", [128, SC, 2], F32)
        EPSC = sb("EPSC", [128, 1], F32)
        PS = es.enter_context(nc.psum_tensor("PS", [128, 8, 512], F32))

        YT = YTf[:].rearrange("p (c t) -> p c t", t=T)
        HN = [YTf[:, 0:D], YTf[:, D:2 * D]]
        GB = YTf[:, 2 * D:4 * D].bitcast(F32)
        AT = [YTf[:, 0:G * T].rearrange("p (g t) -> p g t", t=T),
              YTf[:, G * T:2 * G * T].rearrange("p (g t) -> p g t", t=T)]
        VLN = VLNf[:].rearrange("p (tc d) -> p tc d", d=c.DA)
        SG = ZC[:, :, 0:T]
        IDF = HT[:].rearrange("p dc t -> p (dc t)")[:, 0:256].bitcast(F32)
        ONESF = ZC[:].rearrange("p a b -> p (a b)")[:, 0:128]
        HTflat = HT[:].rearrange("p dc t -> p (dc t)")

        def yres(ch):
            q4 = MIXC // 4
            return "YA" if ch < q4 else ("YB" if ch < 2 * q4 else "YC")

        HNRES = ["YA", "YB"]
        ATRES = ["YA", "YB"]

        slot_elems = SLOT
        n_extra = (TC * c.DA) // slot_elems
        n_extra = min(n_extra, 2)
        slots = [RING[:, i, :] for i in range(c.R)] + \
                [VLNf[:, i * slot_elems:(i + 1) * slot_elems] for i in range(n_extra)]
        slot_res = ["S%d" % i for i in range(c.R)] + ["V%d" % i for i in range(n_extra)]
        VLN_RES = ["V%d" % i for i in range(n_extra)] + ["VLNrest"]

        def bank(b):
            return PS[:, b, :]

        def bank_bf(b):
            return PS[:, b, :].bitcast(BF16).rearrange("p (j c) -> p j c", c=128)

        XRf = XRES[:].rearrange("p tc d -> p (tc d)")
        spt = max(1, (D * 2) // slot_elems)
        xs_f32 = (D // spt)
        assert xs_f32 * 2 >= slot_elems
        n_x = TC * spt
        x_slot_ids = list(range(len(slots), len(slots) + n_x))
        for j in range(n_x):
            slots.append(XRf[:, j * xs_f32:j * xs_f32 + slot_elems // 2].bitcast(BF16))
            slot_res.append("XS%d" % j)
        v_slot_ids = list(range(c.R, c.R + n_extra))
        base_ids = list(range(c.R))
        MODES = {"early": base_ids + x_slot_ids, "late": base_ids + v_slot_ids, "base": base_ids}

        def xr(tc):
            return ["XS%d" % (tc * spt + j) for j in range(spt)]

        sch = Sched()
        ring_state = {"i": 0}

        def next_slot(mode):
            ids = MODES[mode]
            i = ids[ring_state["i"] % len(ids)]
            ring_state["i"] += 1
            return i

        col = lambda a, n=1: COLS[:, a:a + n]

        sch.add("sp", lambda e: [e.dma_start(out=GB, in_=gains_d[0].partition_broadcast(128))],
                writes=["YC"], dma_key="gb")
        sch.add("sp", lambda e: [e.dma_start(out=XRES[:, 0, :], in_=x_d[0:128, :])], writes=xr(0), dma_key="x0")
        sch.add("sp", lambda e: [e.dma_start(out=COLS[:], in_=cols_d)], writes=["COLS"], dma_key="cols")
        WS32 = TMP[:, 0:4, :].rearrange("p a b -> p (a b)")[:, 0:NHA * 128].rearrange("p (h s) -> p h s", s=128) \
            if NHA * 128 <= 4 * SW else None
        assert WS32 is not None
        BSB = ZC[:].rearrange("p a b -> p (a b)")[:, 0:NHA * 128].rearrange("p (h t) -> p h t", t=128)
        assert NHA * 128 <= SC * (T + 2)
        sch.add("sp", lambda e: [e.dma_start(out=WS32, in_=wsp_d.rearrange("h t s -> t h s"))],
                writes=["T0", "T1", "T2", "T3"], dma_key="wsp")
        sch.add("sp", lambda e: [e.dma_start(out=BSB, in_=bsp_d[0].partition_broadcast(128).rearrange("p (h t) -> p h t", t=128))],
                writes=["ZC"], dma_key="bsp")
        sch.add("pool", lambda e: e.memset(IDF, 0.0), writes=["IDF"])
        sch.add("pool", lambda e: e.affine_select(out=IDF, in_=IDF, pattern=[[-1, 128]],
                                                  compare_op=ALU.not_equal, fill=1.0, base=0,
                                                  channel_multiplier=1), reads=["IDF"], writes=["IDF"])
        sch.add("dve", lambda e: e.tensor_copy(out=IDENT[:], in_=IDF), reads=["IDF"], writes=["IDENT"])
        sch.add("dve", lambda e: e.memset(ONESB[:], 1.0), writes=["ONESB"])
        sch.add("dve", lambda e: e.memset(ZSAVE[:], 0.0), writes=["ZSAVE"])
        sch.add("dve", lambda e: e.memset(EPSC[:], EPS), writes=["EPSC"])
        sch.add("pool", lambda e: e.affine_select(out=WS32, in_=WS32, pattern=[[0, NHA], [-1, 128]],
                                                  compare_op=ALU.is_ge, fill=0.0, base=0,
                                                  channel_multiplier=1),
                reads=["T0", "T1", "T2", "T3"], writes=["T0", "T1", "T2", "T3"])
        WSB = TMP[:, 4:6, :].rearrange("p a b -> p (a b)").bitcast(BF16)[:, 0:NHA * 128].rearrange("p (h s) -> p h s", s=128)
        sch.add("dve", lambda e: e.tensor_copy(out=WSB, in_=WS32), reads=["T0", "T1", "T2", "T3"], writes=["T4", "T5"])
        for h0 in range(0, NHA, 8):
            hs = list(range(h0, min(h0 + 8, NHA)))
            def f(e, hs=hs):
                r = None
                for j, h in enumerate(hs):
                    r = e.transpose(out=bank_bf(6)[:, j, :], in_=WSB[:, h, :], identity=IDENT[:])
                return r
            sch.add("pe", f, reads=["T4", "T5", "IDENT"], writes=["P6"])
            sch.add("act", lambda e, hs=hs: e.copy(out=WST[:, hs[0]:hs[-1] + 1, :], in_=bank_bf(6)[:, 0:len(hs), :]),
                    reads=["P6"], writes=["WST"])
        for h0 in range(0, NHA, 4):
            hs = list(range(h0, min(h0 + 4, NHA)))
            def f(e, hs=hs):
                r = None
                for j, h in enumerate(hs):
                    r = e.matmul(out=bank(7)[:, j * 128:(j + 1) * 128], lhsT=ONESB[:], rhs=WST[:, h, :],
                                 start=True, stop=True)
                return r
            sch.add("pe", f, reads=["WST", "ONESB"], writes=["P7"])
            for j, h in enumerate(hs):
                sch.add("dve", lambda e, j=j, h=h: e.scalar_tensor_tensor(
                    out=CB[:, h, :], in0=bank(7)[:, j * 128:(j + 1) * 128], scalar=col(c.c_lnvb + h),
                    in1=BSB[:, h, :], op0=ALU.mult, op1=ALU.add),
                    reads=["P7", "ZC", "COLS"], writes=["CB"])

        stc = {"i": 0}

        def stcol(n=1):
            i = stc["i"]
            if i + n > 320:
                i = 0
            stc["i"] = i + n
            return i

        def rsqrt_ops(dst, src, scale, rd, wr):
            sch.add("act", lambda e: e.activation(out=dst, in_=src, func=AF.Sqrt, bias=EPSC[:, 0:1], scale=scale),
                    reads=list(rd) + ["EPSC"], writes=[wr])
            sch.add("dve", lambda e: e.reciprocal(out=dst, in_=dst), reads=[wr], writes=[wr])

        def emit_norm(ti, gain_row, phase, load_gain=True):
            if load_gain:
                sch.add("sp", lambda e: [e.dma_start(out=GB, in_=gains_d[gain_row].partition_broadcast(128))],
                        writes=["YC"], dma_key="gb")
            JUNK = VLNf[:, 0:D]
            info = {}

            def s1(tc):
                ci = stcol(2)
                ss, rr = ST[:, ci:ci + 1], ST[:, ci + 1:ci + 2]
                sres = "st%d" % ci
                info[tc] = (rr, sres)
                sch.add("act", lambda e, tc=tc, ss=ss: e.activation(
                    out=JUNK, in_=XRES[:, tc, :], func=AF.Square, accum_out=ss),
                    reads=xr(tc), writes=VLN_RES + [sres])
                rsqrt_ops(rr, ss, 1.0 / D, [sres], sres + "r")

            def s2(tc):
                k = tc % 2
                rr, sres = info[tc]
                sch.add("dve", lambda e, tc=tc, k=k, rr=rr: e.scalar_tensor_tensor(
                    out=HN[k], in0=XRES[:, tc, :], scalar=rr, in1=GB, op0=ALU.mult, op1=ALU.mult),
                    reads=xr(tc) + [sres + "r", "YC"], writes=[HNRES[k]])

            def s3(tc):
                k = tc % 2
                for q in range(NPQ):
                    pb = 6 + (q % 2)
                    def f(e, q=q, k=k, pb=pb):
                        r = None
                        for j in range(KP):
                            dc = q * KP + j
                            r = e.transpose(out=bank_bf(pb)[:, j, :], in_=HN[k][:, dc * 128:(dc + 1) * 128],
                                            identity=IDENT[:])
                        return r
                    sch.add("pe", f, reads=[HNRES[k], "IDENT"], writes=["P%d" % pb])
                    eng = "act" if (q % 2 == 0) else "dve"
                    def g(e, q=q, tc=tc, pb=pb, eng=eng):
                        o = HT[:, q * KP:(q + 1) * KP, tc * 128:(tc + 1) * 128]
                        i = bank_bf(pb)[:, 0:KP, :]
                        return e.copy(out=o, in_=i) if eng == "act" else e.tensor_copy(out=o, in_=i)
                    sch.add(eng, g, reads=["P%d" % pb], writes=["HT%d_%d" % (q, tc)])

            s1(0)
            for tc in range(TC):
                if tc + 1 < TC:
                    s1(tc + 1)
                s2(tc)
                if tc >= 1:
                    s3(tc - 1)
            s3(TC - 1)

        def ht_res(q, tc=None):
            if tc is None:
                return ["HT%d_%d" % (q, t) for t in range(TC)]
            return ["HT%d_%d" % (q, tc)]

        def weight_dma(slot_i, runs):
            def f(e):
                return [e.dma_start(out=o, in_=i) for (o, i) in runs]
            sch.add("pool", f, writes=[slot_res[slot_i]], dma_key="w%d" % slot_i, n_dma=len(runs))

        def slot3(slot_i, nk, width):
            return slots[slot_i][:, 0:nk * width].rearrange("p (k w) -> p k w", w=width)

        def proj_fm(src_v, col_runs, nch, mode, extra_after_piece=None):
            width = sum(n for (_, n) in col_runs)
            for q in range(NPQ):
                si = next_slot(mode)
                s3 = slot3(si, KP, width)
                runs, off = [], 0
                for (c0, n) in col_runs:
                    runs.append((s3[:, :, off:off + n], src_v[:, q * KP:(q + 1) * KP, c0:c0 + n]))
                    off += n
                weight_dma(si, runs)
                for ci in range(nch):
                    def f(e, q=q, ci=ci, s3=s3):
                        r = None
                        for j in range(KP):
                            r = e.matmul(out=bank(ci)[:, 0:T], lhsT=s3[:, j, ci * 128:(ci + 1) * 128],
                                         rhs=HT[:, q * KP + j, :],
                                         start=(q == 0 and j == 0), stop=(q == NPQ - 1 and j == KP - 1))
                        return r
                    sch.add("pe", f, reads=[slot_res[si]] + ht_res(q), writes=["P%d" % ci])
                if extra_after_piece is not None:
                    extra_after_piece(q, s3, si)

        for ti in range(c.NTILE):
            r0 = ti * T
            stc["i"] = 0
            for tc in range(TC):
                if ti == 0 and tc == 0:
                    continue
                sch.add("sp", lambda e, tc=tc, r0=r0: [e.dma_start(out=XRES[:, tc, :],
                                                                  in_=x_d[r0 + tc * 128:r0 + (tc + 1) * 128, :])],
                        writes=xr(tc), dma_key="x%d" % tc)
            emit_norm(ti, 0, "mix", load_gain=(ti != 0))

            if ti == 0:
                sch.add("sp", lambda e: [e.dma_start(out=XH[:], in_=xh_d)], writes=["XH"], dma_key="xh")
                sch.add("act", lambda e: e.activation(out=XHS[:], in_=XH[:], func=AF.Square), reads=["XH"], writes=["XHS"])
                hci = stcol(4)
                hs_, hr_ = ST[:, hci:hci + 2], ST[:, hci + 2:hci + 4]
                sch.add("dve", lambda e: e.tensor_reduce(out=hs_, in_=XHS[:].rearrange("p dc j -> p j dc"),
                                                         axis=AX.X, op=ALU.add), reads=["XHS"], writes=["sth"])
                sch.add("dve", lambda e: e.memset(ONESF, 1.0), writes=["ZC"])
                sch.add("pe", lambda e: e.matmul(out=bank(6)[:, 0:2], lhsT=ONESF, rhs=hs_, start=True, stop=True),
                        reads=["sth", "ZC"], writes=["P6"])
                rsqrt_ops(hr_, bank(6)[:, 0:2], 1.0 / D, ["P6"], "sthr")
                sch.add("dve", lambda e: e.tensor_tensor(
                    out=XHS[:], in0=XH[:], in1=col(c.c_gmix, DC).unsqueeze(2).broadcast_to([128, DC, 2]), op=ALU.mult),
                    reads=["XH", "COLS", "XHS"], writes=["XHS"])
                sch.add("dve", lambda e: e.tensor_tensor(
                    out=HTH[:], in0=XHS[:], in1=hr_.unsqueeze(1).broadcast_to([128, DC, 2]), op=ALU.mult),
                    reads=["XHS", "sthr"], writes=["HTH"])

            pend = []

            def flush_pend(n_keep=0):
                while len(pend) > n_keep:
                    pend.pop(0)()

            def stats_op(k2, bankno, colbase, first):
                def f(e):
                    r = None
                    for tc in range(TC):
                        r = e.matmul(out=bank(bankno)[:, colbase + tc:colbase + tc + 1],
                                     lhsT=YSQ[:, k2, tc * 128:(tc + 1) * 128], rhs=ONESB[:, 0:1],
                                     start=(first and tc == 0), stop=False, skip_group_check=True)
                    return r
                sch.add("pe", f, reads=["YSQ%d" % k2, "ONESB"], writes=["P%d" % bankno])

            hook = {"f": None}
            for vj in range(NHA // SC):
                c0 = c.DA + vj * SW
                for q in range(NPQ):
                    si = next_slot("early")
                    s3 = slot3(si, KP, SW)
                    weight_dma(si, [(s3, win_v[:, q * KP:(q + 1) * KP, c0:c0 + SW])])
                    for tc in range(TC):
                        def f(e, q=q, tc=tc, s3=s3):
                            r = None
                            for j in range(KP):
                                r = e.matmul(out=bank(tc)[:, 0:SW], lhsT=HT[:, q * KP + j, tc * 128:(tc + 1) * 128],
                                             rhs=s3[:, j, :], start=(q == 0 and j == 0),
                                             stop=(q == NPQ - 1 and j == KP - 1))
                            return r
                        sch.add("pe", f, reads=[slot_res[si]] + ht_res(q, tc), writes=["P%d" % tc])
                nst = TC * SC
                ci0 = stcol(4 * nst)
                S1, S2 = ST[:, ci0:ci0 + nst], ST[:, ci0 + nst:ci0 + 2 * nst]
                MQ, RS = ST[:, ci0 + 2 * nst:ci0 + 3 * nst], ST[:, ci0 + 3 * nst:ci0 + 4 * nst]
                rn = "sv%d" % ci0
                for tc in range(TC):
                    sch.add("act", lambda e, tc=tc: e.activation(out=TMP[:, tc, :], in_=bank(tc)[:, 0:SW], func=AF.Gelu_apprx_tanh),
                            reads=["P%d" % tc], writes=["T%d" % tc])
                for tc in range(TC):
                    k = tc % 2
                    vg3 = TMP[:, tc, :].rearrange("p (h d) -> p h d", d=128)
                    vsq3 = TMP[:, 4 + k, :].rearrange("p (h d) -> p h d", d=128)
                    sch.add("act", lambda e, tc=tc, k=k: e.activation(out=TMP[:, 4 + k, :], in_=TMP[:, tc, :], func=AF.Square),
                            reads=["T%d" % tc], writes=["T%d" % (4 + k)])
                    sch.add("dve", lambda e, tc=tc, vg3=vg3, S1=S1: e.tensor_reduce(out=S1[:, tc * SC:(tc + 1) * SC], in_=vg3, axis=AX.X, op=ALU.add),
                            reads=["T%d" % tc], writes=[rn + "a"])
                    sch.add("dve", lambda e, tc=tc, vsq3=vsq3, S2=S2: e.tensor_reduce(out=S2[:, tc * SC:(tc + 1) * SC], in_=vsq3, axis=AX.X, op=ALU.add),
                            reads=["T%d" % (4 + k)], writes=[rn + "b"])
                sch.add("dve", lambda e, S1=S1: e.tensor_scalar(out=S1, in0=S1, scalar1=1.0 / 128, scalar2=None, op0=ALU.mult),
                        reads=[rn + "a"], writes=[rn + "a"])
                sch.add("dve", lambda e, S1=S1, MQ=MQ: e.tensor_tensor(out=MQ, in0=S1, in1=S1, op=ALU.mult),
                        reads=[rn + "a"], writes=[rn + "c"])
                sch.add("dve", lambda e, RS=RS, S2=S2, MQ=MQ: e.scalar_tensor_tensor(
                    out=RS, in0=S2, scalar=1.0 / 128, in1=MQ, op0=ALU.mult, op1=ALU.subtract),
                    reads=[rn + "b", rn + "c"], writes=[rn + "d"])
                rsqrt_ops(RS, RS, 1.0, [rn + "d"], rn + "d")
                for tc in range(TC):
                    vg3 = TMP[:, tc, :].rearrange("p (h d) -> p h d", d=128)
                    sch.add("dve", lambda e, vg3=vg3, tc=tc, S1=S1: e.tensor_tensor(
                        out=vg3, in0=vg3, in1=S1[:, tc * SC:(tc + 1) * SC].unsqueeze(2).broadcast_to([128, SC, 128]), op=ALU.subtract),
                        reads=["T%d" % tc, rn + "a"], writes=["T%d" % tc])
                    sch.add("dve", lambda e, vg3=vg3, tc=tc, vj=vj, RS=RS: e.tensor_tensor(
                        out=VLN[:, tc, vj * SW:(vj + 1) * SW].rearrange("p (h d) -> p h d", d=128), in0=vg3,
                        in1=RS[:, tc * SC:(tc + 1) * SC].unsqueeze(2).broadcast_to([128, SC, 128]), op=ALU.mult),
                        reads=["T%d" % tc, rn + "d"], writes=VLN_RES)

            def after_piece1(q, s3, si):
                if q == min(1, NPQ - 1):
                    flush_pend(0)

            for uj in range(NHA // SC):
                c0 = uj * SW
                proj_fm(win_v, [(c0, SW)], SC, "early", after_piece1)
                for ci in range(SC):
                    sch.add("act", lambda e, ci=ci: e.activation(out=TMP[:, ci, :][:, 0:T], in_=bank(ci)[:, 0:T], func=AF.Gelu_apprx_tanh),
                            reads=["P%d" % ci], writes=["T%d" % ci])
                for ci in range(SC):
                    h = uj * SC + ci
                    k = h % 2
                    gu, zz = TMP[:, ci, :][:, 0:T], TMP[:, 4 + k, :][:, 0:T]
                    sb_ = 4 + k
                    def f(e, h=h, sb_=sb_):
                        r = None
                        for tc in range(TC):
                            r = e.matmul(out=bank(sb_)[:, tc * 128:(tc + 1) * 128], lhsT=VLN[:, tc, h * 128:(h + 1) * 128],
                                         rhs=WST[:, h, :], start=True, stop=True)
                        return r
                    sch.add("pe", f, reads=VLN_RES + ["WST"], writes=["P%d" % sb_])
                    zz3 = zz.rearrange("p (tc t) -> p tc t", t=128)
                    sch.add("dve", lambda e, h=h, sb_=sb_, zz3=zz3: e.scalar_tensor_tensor(
                        out=zz3, in0=bank(sb_)[:, 0:T].rearrange("p (tc t) -> p tc t", t=128), scalar=col(c.c_lnvg + h),
                        in1=CB[:, h, :].unsqueeze(1).broadcast_to([128, TC, 128]), op0=ALU.mult, op1=ALU.add),
                        reads=["P%d" % sb_, "CB", "COLS"], writes=["T%d" % (4 + k)])
                    sch.add("dve", lambda e, gu=gu, zz=zz: e.tensor_tensor(out=zz, in0=gu, in1=zz, op=ALU.mult),
                            reads=["T%d" % ci, "T%d" % (4 + k)], writes=["T%d" % (4 + k)])
                    sch.add("act", lambda e, h=h, zz=zz: e.mul(out=YT[:, h, :], in_=zz, mul=col(c.c_ga + h)),
                            reads=["T%d" % (4 + k), "COLS"], writes=[yres(h)])
                    sch.add("act", lambda e, zz=zz, ci=ci: e.activation(out=YSQ[:, ci, :], in_=zz, func=AF.Square),
                            reads=["T%d" % (4 + k)], writes=["YSQ%d" % ci])
                    pend.append(lambda ci=ci, first=(h == 0): stats_op(ci, 7, 0, first))

            hb = SC // 2
            nbj = NCB // SC
            nb_early = nbj // 2
            first_cx = True
            for bj in range(nbj):
                bmode = "early" if bj < nb_early else "late"
                if bj == nb_early:
                    for tc in range(TC):
                        sch.add("sp", lambda e, tc=tc, r0=r0: [e.dma_start(
                            out=XRES[:, tc, :], in_=x_d[r0 + tc * 128:r0 + (tc + 1) * 128, :])],
                            writes=xr(tc), dma_key="x%d" % tc)
                for half in range(2):
                    cxk = bj * 2 + half
                    cC = 2 * c.DA + c.DB + cxk * hb * 128
                    cX = 2 * c.DA + 2 * c.DB + cxk * hb * 128

                    def extra(q, s3, si, cxk=cxk):
                        after_piece1(q, s3, si)
                        if ti != 0:
                            return
                        def f(e):
                            r = None
                            for ci in range(SC):
                                for j in range(KP):
                                    r = e.matmul(out=bank(6)[:, 2 * ci:2 * ci + 2], lhsT=s3[:, j, ci * 128:(ci + 1) * 128],
                                                 rhs=HTH[:, q * KP + j, :],
                                                 start=(q == 0 and ci == 0 and j == 0), stop=False,
                                                 skip_group_check=True)
                            return r
                        sch.add("pe", f, reads=[slot_res[si], "HTH"], writes=["P6"])
                    proj_fm(win_v, [(cC, hb * 128), (cX, hb * 128)], SC, bmode, extra)
                    if first_cx:
                        first_cx = False
                        cA = stcol(TC)
                        RA = ST[:, cA:cA + TC]
                        rsqrt_ops(RA, bank(7)[:, 0:TC], 1.0 / c.DA, ["P7"], "RA")
                    for i in range(hb):
                        sch.add("act", lambda e, i=i: e.copy(out=TMP[:, i, :][:, 0:T], in_=bank(i)[:, 0:T]),
                                reads=["P%d" % i], writes=["T%d" % i])
                    if ti == 0:
                        sch.add("act", lambda e: e.copy(out=CH[:, 0:hb, :], in_=bank(6)[:, 0:2 * hb].rearrange("p (a b) -> p a b", b=2)),
                                reads=["P6"], writes=["CH"])
                    for i in range(hb):
                        zi = half * hb + i
                        cglob = cxk * hb + i
                        ct = TMP[:, i, :][:, 0:T]
                        sch.add("dve", lambda e, i=i, zi=zi, ct=ct: e.tensor_tensor(
                            out=ZC[:, zi, 2:T + 2], in0=bank(hb + i)[:, 0:T], in1=ct, op=ALU.mult),
                            reads=["P%d" % (hb + i), "T%d" % i], writes=["ZC"])
                        if ti == 0:
                            sch.add("dve", lambda e, i=i, zi=zi: e.tensor_tensor(
                                out=ZC[:, zi, 0:2], in0=bank(6)[:, 2 * (hb + i):2 * (hb + i) + 2], in1=CH[:, i, :], op=ALU.mult),
                                reads=["P6", "CH"], writes=["ZC"])
                        else:
                            sch.add("dve", lambda e, zi=zi, cglob=cglob: e.tensor_copy(out=ZC[:, zi, 0:2], in_=ZSAVE[:, cglob, :]),
                                    reads=["ZSAVE"], writes=["ZC"])
                        sch.add("dve", lambda e, zi=zi, cglob=cglob: e.tensor_copy(out=ZSAVE[:, cglob, :], in_=ZC[:, zi, T:T + 2]),
                                reads=["ZC", "ZSAVE"], writes=["ZSAVE"])
                for ci in range(SC):
                    cb = bj * SC + ci
                    t1 = TMP[:, 2 + ci, :][:, 0:T]
                    w = lambda kk, cb=cb: col(c.c_cw + kk * NCB + cb)
                    sch.add("act", lambda e, ci=ci, t1=t1, w=w: e.mul(out=t1, in_=ZC[:, ci, 0:T], mul=w(0)),
                            reads=["ZC", "COLS"], writes=["T%d" % (2 + ci)])
                    sch.add("dve", lambda e, ci=ci, t1=t1, w=w: e.scalar_tensor_tensor(
                        out=t1, in0=ZC[:, ci, 1:T + 1], scalar=w(1), in1=t1, op0=ALU.mult, op1=ALU.add),
                        reads=["ZC", "COLS", "T%d" % (2 + ci)], writes=["T%d" % (2 + ci)])
                    sch.add("dve", lambda e, ci=ci, t1=t1, w=w: e.scalar_tensor_tensor(
                        out=t1, in0=ZC[:, ci, 2:T + 2], scalar=w(2), in1=t1, op0=ALU.mult, op1=ALU.add),
                        reads=["ZC", "COLS", "T%d" % (2 + ci)], writes=["T%d" % (2 + ci)])
                cB = 2 * c.DA + bj * SW
                proj_fm(win_v, [(cB, SW)], SC, bmode, after_piece1)
                for ci in range(SC):
                    t1 = TMP[:, 2 + ci, :][:, 0:T]
                    sch.add("dve", lambda e, ci=ci, t1=t1: e.tensor_tensor(out=t1, in0=bank(ci)[:, 0:T], in1=t1, op=ALU.mult),
                            reads=["P%d" % ci, "T%d" % (2 + ci)], writes=["T%d" % (2 + ci)])
                for ci in range(SC):
                    cb = bj * SC + ci
                    t1 = TMP[:, 2 + ci, :][:, 0:T]
                    sch.add("act", lambda e, cb=cb, t1=t1: e.mul(out=YT[:, NHA + cb, :], in_=t1, mul=col(c.c_gb + cb)),
                            reads=["T%d" % (2 + ci), "COLS"], writes=[yres(NHA + cb)])
                    sch.add("act", lambda e, t1=t1, ci=ci: e.activation(out=YSQ[:, ci, :], in_=t1, func=AF.Square),
                            reads=["T%d" % (2 + ci)], writes=["YSQ%d" % ci])
                    pend.append(lambda ci=ci, first=(cb == 0): stats_op(ci, 4, 0, first))

            cBc = stcol(TC)
            RB = ST[:, cBc:cBc + TC]
            rb_done = False
            for ds in range(c.DS):
                d0 = ds * c.DSW
                for part, (cc0, ncc, RR, rname) in enumerate(((0, NHA, RA, "RA"), (NHA, NCB, RB, "RB"))):
                    npq = ncc // KP
                    for q in range(npq):
                        si = next_slot("late")
                        s3 = slot3(si, KP, c.DSW)
                        weight_dma(si, [(s3, wout_v[:, cc0 + q * KP:cc0 + (q + 1) * KP, d0:d0 + c.DSW])])
                        for tc in range(TC):
                            def f(e, q=q, tc=tc, s3=s3, cc0=cc0, npq=npq):
                                r = None
                                for j in range(KP):
                                    r = e.matmul(out=bank(tc)[:, 0:c.DSW], lhsT=YT[:, cc0 + q * KP + j, tc * 128:(tc + 1) * 128],
                                                 rhs=s3[:, j, :], start=(q == 0 and j == 0),
                                                 stop=(q == npq - 1 and j == KP - 1))
                                return r
                            yr = sorted(set(yres(cc0 + q * KP + j) for j in range(KP)))
                            sch.add("pe", f, reads=[slot_res[si]] + yr, writes=["P%d" % tc])
                        if not rb_done:
                            rb_done = True
                            flush_pend(0)
                            rsqrt_ops(RB, bank(4)[:, 0:TC], 1.0 / c.DB, ["P4"], "RB")
                    for tc in range(TC):
                        sch.add("dve", lambda e, tc=tc, RR=RR, d0=d0: e.scalar_tensor_tensor(
                            out=XRES[:, tc, d0:d0 + c.DSW], in0=bank(tc)[:, 0:c.DSW], scalar=RR[:, tc:tc + 1],
                            in1=XRES[:, tc, d0:d0 + c.DSW], op0=ALU.mult, op1=ALU.add),
                            reads=["P%d" % tc, rname] + xr(tc), writes=xr(tc))

            emit_norm(ti, 1, "ffn")

            NG = (FC + G - 1) // G
            dbank = {"i": 0}
            deferred_down = []

            def emit_down(gj, nchg, k):
                for ds in range(c.DS):
                    d0 = ds * c.DSW
                    si = next_slot("late")
                    s3 = slot3(si, nchg, c.DSW)
                    weight_dma(si, [(s3, wd_v[:, gj * G:gj * G + nchg, d0:d0 + c.DSW])])
                    for tc in range(TC):
                        b = 4 + (dbank["i"] % 4); dbank["i"] += 1
                        def f(e, tc=tc, s3=s3, b=b, nchg=nchg, k=k):
                            r = None
                            for gi in range(nchg):
                                r = e.matmul(out=bank(b)[:, 0:c.DSW], lhsT=AT[k][:, gi, tc * 128:(tc + 1) * 128],
                                             rhs=s3[:, gi, :], start=(gi == 0), stop=(gi == nchg - 1))
                            return r
                        sch.add("pe", f, reads=[slot_res[si], ATRES[k]], writes=["P%d" % b])
                        sch.add("dve", lambda e, tc=tc, b=b, d0=d0: e.tensor_tensor(
                            out=XRES[:, tc, d0:d0 + c.DSW], in0=bank(b)[:, 0:c.DSW], in1=XRES[:, tc, d0:d0 + c.DSW], op=ALU.add),
                            reads=["P%d" % b] + xr(tc), writes=xr(tc))

            for gj in range(NG):
                f0 = gj * G
                nchg = min(G, FC - f0)
                k = gj % 2
                for s0 in range(0, nchg, SC):
                    nch = min(SC, nchg - s0)
                    cF = (f0 + s0) * 128
                    proj_fm(wg_v, [(cF, nch * 128)], nch, "late")
                    for ci in range(nch):
                        sch.add("act", lambda e, ci=ci: e.activation(out=SG[:, ci, :], in_=bank(ci)[:, 0:T], func=AF.Silu),
                                reads=["P%d" % ci], writes=["SG%d" % ci])
                    proj_fm(wu_v, [(cF, nch * 128)], nch, "late")
                    for ci in range(nch):
                        sch.add("dve", lambda e, ci=ci, k=k, s0=s0: e.tensor_tensor(
                            out=AT[k][:, s0 + ci, :], in0=bank(ci)[:, 0:T], in1=SG[:, ci, :], op=ALU.mult),
                            reads=["P%d" % ci, "SG%d" % ci], writes=[ATRES[k]])
                    if s0 == 0 and deferred_down:
                        deferred_down.pop(0)()
                deferred_down.append(lambda gj=gj, nchg=nchg, k=k: emit_down(gj, nchg, k))
                if gj == 0:
                    pass
            while deferred_down:
                deferred_down.pop(0)()

            sch.add("sp", lambda e: [e.dma_start(out=GB, in_=gains_d[2].partition_broadcast(128))],
                    writes=["YC"], dma_key="gb")
            finfo = {}
            allht = [r for q in range(NPQ) for r in ht_res(q)]
            for tc in range(TC):
                ci = stcol(2)
                ss, rr = ST[:, ci:ci + 1], ST[:, ci + 1:ci + 2]
                sres = "sf%d" % ci
                finfo[tc] = (rr, sres)
                sch.add("act", lambda e, tc=tc, ss=ss: e.activation(out=HTflat[:, 0:D], in_=XRES[:, tc, :], func=AF.Square,
                                                                    accum_out=ss),
                        reads=xr(tc), writes=allht + [sres])
                rsqrt_ops(rr, ss, 1.0 / D, [sres], sres + "r")
            for tc in range(TC):
                rr, sres = finfo[tc]
                sch.add("dve", lambda e, tc=tc, rr=rr: e.scalar_tensor_tensor(
                    out=XRES[:, tc, :], in0=XRES[:, tc, :], scalar=rr, in1=GB, op0=ALU.mult, op1=ALU.mult),
                    reads=xr(tc) + [sres + "r", "YC"], writes=xr(tc))
                sch.add("sp", lambda e, tc=tc, r0=r0: [e.dma_start(out=out_d[r0 + tc * 128:r0 + (tc + 1) * 128, :], in_=XRES[:, tc, :])],
                        reads=xr(tc), dma_key="o%d" % tc)

        sems = {}
        for n in sch.sem_names():
            sems[n] = es.enter_context(nc.semaphore(n.replace(":", "_")))
        with nc.allow_low_precision("bf16 matmul operands, fp32 PSUM accumulation"):
            with nc.Block() as block:
                @block.tensor
                def _(e):
                    sch.emit("pe", e, sems, attach=True)

                @block.scalar
                def _(e):
                    sch.emit("act", e, sems)

                @block.vector
                def _(e):
                    sch.emit("dve", e, sems)

                @block.gpsimd
                def _(e):
                    sch.emit("pool", e, sems)

                @block.sync
                def _(e):
                    sch.emit("sp", e, sems)
                    for tc in range(TC):
                        e.wait_ge(sems["d:o%d" % tc], 16 * sch.dcnt["o%d" % tc])
    return nc


def make_in_maps(cfg, x2d, p):
    c = cfg
    rows = c.NTILE * c.T
    cols = np.zeros((128, c.NCOLS), np.float32)
    cols[:, c.c_lnvg:c.c_lnvg + c.NHA] = p["ln_v_g"].reshape(c.NHA, 128).T
    cols[:, c.c_lnvb:c.c_lnvb + c.NHA] = p["ln_v_b"].reshape(c.NHA, 128).T
    cols[:, c.c_ga:c.c_ga + c.NHA] = p["out_norm_a_g"].reshape(c.NHA, 128).T
    cols[:, c.c_gb:c.c_gb + c.NCB] = p["out_norm_b_g"].reshape(c.NCB, 128).T
    cols[:, c.c_cw:c.c_cw + 3 * c.NCB] = p["conv_w"].reshape(3, c.NCB, 128).transpose(2, 0, 1).reshape(128, 3 * c.NCB)
    cols[:, c.c_gmix:c.c_gmix + c.DC] = p["mix_norm_g"].reshape(c.DC, 128).T
    gains = np.ascontiguousarray(np.stack([p["mix_norm_g"], p["ffn_norm_g"], p["final_norm_g"]]).astype(np.float32))
    shared = {
        "cols": cols, "gains": gains,
        "w_spatial": np.ascontiguousarray(p["w_spatial"], dtype=np.float32),
        "b_spatial": np.ascontiguousarray(p["b_spatial"].reshape(1, -1), dtype=np.float32),
        "w_in": np.ascontiguousarray(p["w_in"], dtype=np.float32),
        "w_out": np.ascontiguousarray(p["w_out"], dtype=np.float32),
        "w_gate": np.ascontiguousarray(p["w_gate"], dtype=np.float32),
        "w_up": np.ascontiguousarray(p["w_up"], dtype=np.float32),
        "w_down": np.ascontiguousarray(p["w_down"], dtype=np.float32),
    }
    in_maps = []
    for i in range(c.NCORES):
        xs = np.ascontiguousarray(x2d[i * rows:(i + 1) * rows])
        halo = np.zeros((2, c.D), np.float32)
        if i > 0:
            halo[:] = x2d[i * rows - 2:i * rows]
        xh = np.ascontiguousarray(halo.reshape(2, c.DC, 128).transpose(2, 1, 0))
        m = dict(shared)
        m["x"] = xs
        m["xh"] = xh
        in_maps.append(m)
    return in_maps


_PROGRAM_CACHE = {}


def kernel(x, mix_norm_g, w_in, ln_v_g, ln_v_b, w_spatial, b_spatial, conv_w,
           out_norm_a_g, out_norm_b_g, w_out, ffn_norm_g, w_gate, w_up, w_down, final_norm_g):
    cfg = Cfg()
    x = np.asarray(x, dtype=np.float32)
    B, S, D = x.shape
    assert D == cfg.D and B * S == cfg.NCORES * cfg.NTILE * cfg.T
    p = {
        "mix_norm_g": np.asarray(mix_norm_g)[0], "w_in": np.asarray(w_in)[0],
        "ln_v_g": np.asarray(ln_v_g)[0], "ln_v_b": np.asarray(ln_v_b)[0],
        "w_spatial": np.asarray(w_spatial)[0], "b_spatial": np.asarray(b_spatial)[0],
        "conv_w": np.asarray(conv_w)[0], "out_norm_a_g": np.asarray(out_norm_a_g)[0],
        "out_norm_b_g": np.asarray(out_norm_b_g)[0], "w_out": np.asarray(w_out)[0],
        "ffn_norm_g": np.asarray(ffn_norm_g)[0], "w_gate": np.asarray(w_gate)[0],
        "w_up": np.asarray(w_up)[0], "w_down": np.asarray(w_down)[0],
        "final_norm_g": np.asarray(final_norm_g),
    }
    in_maps = make_in_maps(cfg, x.reshape(B * S, D), p)
    if "nc" not in _PROGRAM_CACHE:
        _PROGRAM_CACHE["nc"] = build_program(cfg)
    nc = _PROGRAM_CACHE["nc"]
    res = run_bass_kernel_spmd(nc, in_maps, core_ids=list(range(cfg.NCORES)))
    out = np.concatenate([np.asarray(r["out"]) for r in res.results], axis=0)
    return out.reshape(B, S, D).astype(np.float32)
```

```python
import numpy as np
from contextlib import ExitStack

import concourse.bass as bass
import concourse.mybir as mybir
from concourse.bass_utils import run_bass_kernel_spmd

F32 = mybir.dt.float32
BF16 = mybir.dt.bfloat16
AF = mybir.ActivationFunctionType
ALU = mybir.AluOpType
AX = mybir.AxisListType
EPS = 1e-6


class Cfg:
    def __init__(self, D=4096, NHA=16, NCB=16, F=11008, T=512, NTILE=2, SC=4, KP=8, R=3,
                 NCORES=8):
        self.D, self.NHA, self.NCB, self.F, self.T, self.NTILE = D, NHA, NCB, F, T, NTILE
        self.SC, self.KP, self.R, self.NCORES = SC, KP, R, NCORES
        self.DC = D // 128
        self.TC = T // 128
        self.FC = F // 128
        self.DA, self.DB = NHA * 128, NCB * 128
        self.MIXC = NHA + NCB
        self.MIX = self.MIXC * 128
        self.INC = 2 * self.DA + 3 * self.DB
        self.SW = SC * 128
        self.NPQ = self.DC // KP
        self.G = self.MIXC // 4
        self.DS = D // 512 if D >= 512 else 1
        self.DSW = min(512, D)
        assert self.MIXC * T == 4 * D, "YT aliasing (hn0|hn1|GB) needs MIXC*T == 4*D"
        assert self.DC % KP == 0 and NHA % KP == 0 and NCB % KP == 0
        assert NHA % SC == 0 and NCB % SC == 0 and SC % 2 == 0
        assert self.G <= KP and self.G % SC == 0
        assert self.TC * self.DA == 2 * KP * self.SW or True
        o = 0
        self.c_lnvg = o; o += NHA
        self.c_lnvb = o; o += NHA
        self.c_ga = o; o += NHA
        self.c_gb = o; o += NCB
        self.c_cw = o; o += 3 * NCB
        self.c_gmix = o; o += self.DC
        self.NCOLS = o


class _Op:
    __slots__ = ("eng", "emit", "deps", "raw", "is_dma", "key", "done", "n_dma")


class Sched:
    ENGS = ("pe", "act", "dve", "pool", "sp")

    def __init__(self):
        self.ops = []
        self.lastw = {}
        self.readers = {}
        self.cnt = {e: 0 for e in self.ENGS}
        self.dcnt = {}

    def add(self, eng, emit, reads=(), writes=(), dma_key=None, n_dma=1):
        idx = len(self.ops)
        deps, raw = set(), set()
        for r in reads:
            if r in self.lastw:
                deps.add(self.lastw[r]); raw.add(self.lastw[r])
            if r[0] == "P":
                for rd in self.readers.get(r, ()):
                    if self.ops[rd].eng != eng:
                        deps.add(rd)
        for w in writes:
            if w in self.lastw:
                deps.add(self.lastw[w]); raw.add(self.lastw[w])
            deps.update(self.readers.get(w, ()))
        op = _Op()
        op.eng, op.emit, op.is_dma, op.key, op.n_dma = eng, emit, dma_key is not None, dma_key, n_dma
        keep = set()
        for d in deps:
            dop = self.ops[d]
            if dop.is_dma:
                keep.add(d)
            elif dop.eng == eng:
                if eng != "pe" and d in raw:
                    keep.add(d)
            else:
                keep.add(d)
        op.deps = keep
        if op.is_dma:
            self.dcnt[dma_key] = self.dcnt.get(dma_key, 0) + n_dma
            op.done = ("d:" + dma_key, 16 * self.dcnt[dma_key])
        else:
            self.cnt[eng] += 1
            op.done = ("e:" + eng, self.cnt[eng])
        for r in reads:
            self.readers.setdefault(r, []).append(idx)
        for w in writes:
            self.lastw[w] = idx
            self.readers[w] = []
        self.ops.append(op)
        return idx

    def sem_names(self):
        names = ["e:" + e for e in self.ENGS]
        names += ["d:" + k for k in self.dcnt]
        return names

    def emit(self, eng_name, eng, sems):
        known = {}
        for op in self.ops:
            if op.eng != eng_name:
                continue
            waits = {}
            for d in op.deps:
                s, v = self.ops[d].done
                if v > waits.get(s, 0):
                    waits[s] = v
            for s, v in waits.items():
                if known.get(s, 0) >= v:
                    continue
                eng.wait_ge(sems[s], v)
                known[s] = v
            res = op.emit(eng)
            if op.is_dma:
                assert len(res) == op.n_dma
                for ins in res:
                    ins.then_inc(sems[op.done[0]], 16)
            else:
                res.then_inc(sems[op.done[0]], 1)


def build_program(cfg):
    c = cfg
    D, T, TC, DC, SC, KP, SW, NPQ = c.D, c.T, c.TC, c.DC, c.SC, c.KP, c.SW, c.NPQ
    NHA, NCB, MIXC, G, FC = c.NHA, c.NCB, c.MIXC, c.G, c.FC
    ROWS = c.NTILE * T

    nc = bass.Bass("TRN2", target_bir_lowering=False)
    x_d = nc.dram_tensor("x", [ROWS, D], F32, kind="ExternalInput").ap()
    xh_d = nc.dram_tensor("xh", [128, DC, 2], F32, kind="ExternalInput").ap()
    cols_d = nc.dram_tensor("cols", [128, c.NCOLS], F32, kind="ExternalInput").ap()
    gains_d = nc.dram_tensor("gains", [3, D], F32, kind="ExternalInput").ap()
    wsp_d = nc.dram_tensor("w_spatial", [NHA, 128, 128], F32, kind="ExternalInput").ap()
    bsp_d = nc.dram_tensor("b_spatial", [1, NHA * 128], F32, kind="ExternalInput").ap()
    win_d = nc.dram_tensor("w_in", [D, c.INC], F32, kind="ExternalInput").ap()
    wout_d = nc.dram_tensor("w_out", [c.MIX, D], F32, kind="ExternalInput").ap()
    wg_d = nc.dram_tensor("w_gate", [D, c.F], F32, kind="ExternalInput").ap()
    wu_d = nc.dram_tensor("w_up", [D, c.F], F32, kind="ExternalInput").ap()
    wd_d = nc.dram_tensor("w_down", [c.F, D], F32, kind="ExternalInput").ap()
    out_d = nc.dram_tensor("out", [ROWS, D], F32, kind="ExternalOutput").ap()

    win_v = win_d.rearrange("(dc p) c -> p dc c", p=128)
    wout_v = wout_d.rearrange("(cc p) d -> p cc d", p=128)
    wg_v = wg_d.rearrange("(dc p) f -> p dc f", p=128)
    wu_v = wu_d.rearrange("(dc p) f -> p dc f", p=128)
    wd_v = wd_d.rearrange("(fc p) d -> p fc d", p=128)

    es = ExitStack()
    with es:
        def sb(name, shape, dt):
            return es.enter_context(nc.sbuf_tensor(name, shape, dt))

        XRES = sb("XRES", [128, TC, D], F32)
        HT = sb("HT", [128, DC, T], BF16)
        YTf = sb("YT", [128, MIXC * T], BF16)
        VLNf = sb("VLN", [128, TC * c.DA], BF16)
        SLOT = KP * max(SW, c.DSW)
        RING = sb("RING", [128, c.R, SLOT], BF16)
        TMP = sb("TMP", [128, 6, SW], F32)
        ZC = sb("ZC", [128, SC, T + 2], F32)
        YSQ = sb("YSQ", [128, SC, T], BF16)
        IDENT = sb("IDENT", [128, 128], BF16)
        ONESB = sb("ONESB", [128, 128], BF16)
        WST = sb("WST", [128, NHA, 128], BF16)
        CB = sb("CB", [128, NHA, 128], F32)
        COLS = sb("COLS", [128, c.NCOLS], F32)
        ST = sb("ST", [128, 320], F32)
        XH = sb("XH", [128, DC, 2], F32)
        XHS = sb("XHS", [128, DC, 2], F32)
        HTH = sb("HTH", [128, DC, 2], BF16)
        ZSAVE = sb("ZSAVE", [128, NCB, 2], F32)
        CH = sb("CH", [128, SC, 2], F32)
        EPSC = sb("EPSC", [128, 1], F32)
        PS = es.enter_context(nc.psum_tensor("PS", [128, 8, 512], F32))

        YT = YTf[:].rearrange("p (c t) -> p c t", t=T)
        HN = [YTf[:, 0:D], YTf[:, D:2 * D]]
        GB = YTf[:, 2 * D:4 * D].bitcast(F32)
        AT = [YTf[:, 0:G * T].rearrange("p (g t) -> p g t", t=T),
              YTf[:, G * T:2 * G * T].rearrange("p (g t) -> p g t", t=T)]
        VLN = VLNf[:].rearrange("p (tc d) -> p tc d", d=c.DA)
        SG = ZC[:, :, 0:T]
        IDF = HT[:].rearrange("p dc t -> p (dc t)")[:, 0:256].bitcast(F32)
        ONESF = ZC[:].rearrange("p a b -> p (a b)")[:, 0:128]
        HTflat = HT[:].rearrange("p dc t -> p (dc t)")

        def yres(ch):
            q4 = MIXC // 4
            return "YA" if ch < q4 else ("YB" if ch < 2 * q4 else "YC")

        HNRES = ["YA", "YB"]
        ATRES = ["YA", "YB"]

        slot_elems = SLOT
        n_extra = (TC * c.DA) // slot_elems
        n_extra = min(n_extra, 2)
        slots = [RING[:, i, :] for i in range(c.R)] + \
                [VLNf[:, i * slot_elems:(i + 1) * slot_elems] for i in range(n_extra)]
        slot_res = ["S%d" % i for i in range(c.R)] + ["V%d" % i for i in range(n_extra)]
        VLN_RES = ["V%d" % i for i in range(n_extra)] + ["VLNrest"]

        def bank(b):
            return PS[:, b, :]

        def bank_bf(b):
            return PS[:, b, :].bitcast(BF16).rearrange("p (j c) -> p j c", c=128)

        XRf = XRES[:].rearrange("p tc d -> p (tc d)")
        spt = max(1, (D * 2) // slot_elems)
        xs_f32 = (D // spt)
        assert xs_f32 * 2 >= slot_elems
        n_x = TC * spt
        x_slot_ids = list(range(len(slots), len(slots) + n_x))
        for j in range(n_x):
            slots.append(XRf[:, j * xs_f32:j * xs_f32 + slot_elems // 2].bitcast(BF16))
            slot_res.append("XS%d" % j)
        v_slot_ids = list(range(c.R, c.R + n_extra))
        base_ids = list(range(c.R))
        MODES = {"early": base_ids + x_slot_ids, "late": base_ids + v_slot_ids, "base": base_ids}

        def xr(tc):
            return ["XS%d" % (tc * spt + j) for j in range(spt)]

        sch = Sched()
        ring_state = {"seq": 0}
        last_use = {}

        def touch(slot_id):
            ring_state["seq"] += 1
            last_use[slot_id] = ring_state["seq"]

        def next_slot(mode):
            i = min(MODES[mode], key=lambda sid: (last_use.get(sid, -1), sid))
            touch(i)
            return i

        col = lambda a, n=1: COLS[:, a:a + n]

        sch.add("sp", lambda e: [e.dma_start(out=GB, in_=gains_d[0].partition_broadcast(128))],
                writes=["YC"], dma_key="gb")
        sch.add("sp", lambda e: [e.dma_start(out=XRES[:, 0, :], in_=x_d[0:128, :])], writes=xr(0), dma_key="x0")
        sch.add("sp", lambda e: [e.dma_start(out=COLS[:], in_=cols_d)], writes=["COLS"], dma_key="cols")
        WS32 = TMP[:, 0:4, :].rearrange("p a b -> p (a b)")[:, 0:NHA * 128].rearrange("p (h s) -> p h s", s=128) \
            if NHA * 128 <= 4 * SW else None
        assert WS32 is not None
        BSB = ZC[:].rearrange("p a b -> p (a b)")[:, 0:NHA * 128].rearrange("p (h t) -> p h t", t=128)
        assert NHA * 128 <= SC * (T + 2)
        sch.add("sp", lambda e: [e.dma_start(out=WS32, in_=wsp_d.rearrange("h t s -> t h s"))],
                writes=["T0", "T1", "T2", "T3"], dma_key="wsp")
        sch.add("sp", lambda e: [e.dma_start(out=BSB, in_=bsp_d[0].partition_broadcast(128).rearrange("p (h t) -> p h t", t=128))],
                writes=["ZC"], dma_key="bsp")
        sch.add("pool", lambda e: e.memset(IDF, 0.0), writes=["IDF"])
        sch.add("pool", lambda e: e.affine_select(out=IDF, in_=IDF, pattern=[[-1, 128]],
                                                  compare_op=ALU.not_equal, fill=1.0, base=0,
                                                  channel_multiplier=1), reads=["IDF"], writes=["IDF"])
        sch.add("dve", lambda e: e.tensor_copy(out=IDENT[:], in_=IDF), reads=["IDF"], writes=["IDENT"])
        sch.add("dve", lambda e: e.memset(ONESB[:], 1.0), writes=["ONESB"])
        sch.add("dve", lambda e: e.memset(ZSAVE[:], 0.0), writes=["ZSAVE"])
        sch.add("dve", lambda e: e.memset(EPSC[:], EPS), writes=["EPSC"])
        sch.add("pool", lambda e: e.affine_select(out=WS32, in_=WS32, pattern=[[0, NHA], [-1, 128]],
                                                  compare_op=ALU.is_ge, fill=0.0, base=0,
                                                  channel_multiplier=1),
                reads=["T0", "T1", "T2", "T3"], writes=["T0", "T1", "T2", "T3"])
        WSB = TMP[:, 4:6, :].rearrange("p a b -> p (a b)").bitcast(BF16)[:, 0:NHA * 128].rearrange("p (h s) -> p h s", s=128)
        sch.add("dve", lambda e: e.tensor_copy(out=WSB, in_=WS32), reads=["T0", "T1", "T2", "T3"], writes=["T4", "T5"])
        for h0 in range(0, NHA, 8):
            hs = list(range(h0, min(h0 + 8, NHA)))
            def f(e, hs=hs):
                r = None
                for j, h in enumerate(hs):
                    r = e.transpose(out=bank_bf(6)[:, j, :], in_=WSB[:, h, :], identity=IDENT[:])
                return r
            sch.add("pe", f, reads=["T4", "T5", "IDENT"], writes=["P6"])
            sch.add("act", lambda e, hs=hs: e.copy(out=WST[:, hs[0]:hs[-1] + 1, :], in_=bank_bf(6)[:, 0:len(hs), :]),
                    reads=["P6"], writes=["WST"])
        for h0 in range(0, NHA, 4):
            hs = list(range(h0, min(h0 + 4, NHA)))
            def f(e, hs=hs):
                r = None
                for j, h in enumerate(hs):
                    r = e.matmul(out=bank(7)[:, j * 128:(j + 1) * 128], lhsT=ONESB[:], rhs=WST[:, h, :],
                                 start=True, stop=True)
                return r
            sch.add("pe", f, reads=["WST", "ONESB"], writes=["P7"])
            for j, h in enumerate(hs):
                sch.add("dve", lambda e, j=j, h=h: e.scalar_tensor_tensor(
                    out=CB[:, h, :], in0=bank(7)[:, j * 128:(j + 1) * 128], scalar=col(c.c_lnvb + h),
                    in1=BSB[:, h, :], op0=ALU.mult, op1=ALU.add),
                    reads=["P7", "ZC", "COLS"], writes=["CB"])

        stc = {"i": 0}

        def stcol(n=1):
            i = stc["i"]
            if i + n > 320:
                i = 0
            stc["i"] = i + n
            return i

        def rsqrt_ops(dst, src, scale, rd, wr):
            sch.add("act", lambda e: e.activation(out=dst, in_=src, func=AF.Sqrt, bias=EPSC[:, 0:1], scale=scale),
                    reads=list(rd) + ["EPSC"], writes=[wr])
            sch.add("dve", lambda e: e.reciprocal(out=dst, in_=dst), reads=[wr], writes=[wr])

        def emit_norm(ti, gain_row, phase, load_gain=True):
            if load_gain:
                sch.add("sp", lambda e: [e.dma_start(out=GB, in_=gains_d[gain_row].partition_broadcast(128))],
                        writes=["YC"], dma_key="gb")
            JUNK = VLNf[:, 0:D]
            info = {}

            def s1(tc):
                ci = stcol(2)
                ss, rr = ST[:, ci:ci + 1], ST[:, ci + 1:ci + 2]
                sres = "st%d" % ci
                info[tc] = (rr, sres)
                sch.add("act", lambda e, tc=tc, ss=ss: e.activation(
                    out=JUNK, in_=XRES[:, tc, :], func=AF.Square, accum_out=ss),
                    reads=xr(tc), writes=VLN_RES + [sres])
                rsqrt_ops(rr, ss, 1.0 / D, [sres], sres + "r")

            def s2(tc):
                k = tc % 2
                rr, sres = info[tc]
                sch.add("dve", lambda e, tc=tc, k=k, rr=rr: e.scalar_tensor_tensor(
                    out=HN[k], in0=XRES[:, tc, :], scalar=rr, in1=GB, op0=ALU.mult, op1=ALU.mult),
                    reads=xr(tc) + [sres + "r", "YC"], writes=[HNRES[k]])

            def s3(tc):
                k = tc % 2
                for q in range(NPQ):
                    pb = 6 + (q % 2)
                    def f(e, q=q, k=k, pb=pb):
                        r = None
                        for j in range(KP):
                            dc = q * KP + j
                            r = e.transpose(out=bank_bf(pb)[:, j, :], in_=HN[k][:, dc * 128:(dc + 1) * 128],
                                            identity=IDENT[:])
                        return r
                    sch.add("pe", f, reads=[HNRES[k], "IDENT"], writes=["P%d" % pb])
                    eng = "act" if (q % 2 == 0) else "dve"
                    def g(e, q=q, tc=tc, pb=pb, eng=eng):
                        o = HT[:, q * KP:(q + 1) * KP, tc * 128:(tc + 1) * 128]
                        i = bank_bf(pb)[:, 0:KP, :]
                        return e.copy(out=o, in_=i) if eng == "act" else e.tensor_copy(out=o, in_=i)
                    sch.add(eng, g, reads=["P%d" % pb], writes=["HT%d_%d" % (q, tc)])

            s1(0)
            for tc in range(TC):
                if tc + 1 < TC:
                    s1(tc + 1)
                s2(tc)
                if tc >= 1:
                    s3(tc - 1)
            s3(TC - 1)

        def ht_res(q, tc=None):
            if tc is None:
                return ["HT%d_%d" % (q, t) for t in range(TC)]
            return ["HT%d_%d" % (q, tc)]

        def weight_dma(slot_i, runs):
            def f(e):
                return [e.dma_start(out=o, in_=i) for (o, i) in runs]
            sch.add("pool", f, writes=[slot_res[slot_i]], dma_key="w%d" % slot_i, n_dma=len(runs))

        def slot3(slot_i, nk, width):
            return slots[slot_i][:, 0:nk * width].rearrange("p (k w) -> p k w", w=width)

        def proj_fm(src_v, col_runs, nch, mode, extra_after_piece=None):
            width = sum(n for (_, n) in col_runs)
            for q in range(NPQ):
                si = next_slot(mode)
                s3 = slot3(si, KP, width)
                runs, off = [], 0
                for (c0, n) in col_runs:
                    runs.append((s3[:, :, off:off + n], src_v[:, q * KP:(q + 1) * KP, c0:c0 + n]))
                    off += n
                weight_dma(si, runs)
                for ci in range(nch):
                    def f(e, q=q, ci=ci, s3=s3):
                        r = None
                        for j in range(KP):
                            r = e.matmul(out=bank(ci)[:, 0:T], lhsT=s3[:, j, ci * 128:(ci + 1) * 128],
                                         rhs=HT[:, q * KP + j, :],
                                         start=(q == 0 and j == 0), stop=(q == NPQ - 1 and j == KP - 1))
                        return r
                    sch.add("pe", f, reads=[slot_res[si]] + ht_res(q), writes=["P%d" % ci])
                if extra_after_piece is not None:
                    extra_after_piece(q, s3, si)

        for ti in range(c.NTILE):
            r0 = ti * T
            stc["i"] = 0
            for sid in x_slot_ids:
                touch(sid)
            for tc in range(TC):
                if ti == 0 and tc == 0:
                    continue
                sch.add("sp", lambda e, tc=tc, r0=r0: [e.dma_start(out=XRES[:, tc, :],
                                                                  in_=x_d[r0 + tc * 128:r0 + (tc + 1) * 128, :])],
                        writes=xr(tc), dma_key="x%d" % tc)
            emit_norm(ti, 0, "mix", load_gain=(ti != 0))

            if ti == 0:
                sch.add("sp", lambda e: [e.dma_start(out=XH[:], in_=xh_d)], writes=["XH"], dma_key="xh")
                sch.add("act", lambda e: e.activation(out=XHS[:], in_=XH[:], func=AF.Square), reads=["XH"], writes=["XHS"])
                hci = stcol(4)
                hs_, hr_ = ST[:, hci:hci + 2], ST[:, hci + 2:hci + 4]
                sch.add("dve", lambda e: e.tensor_reduce(out=hs_, in_=XHS[:].rearrange("p dc j -> p j dc"),
                                                         axis=AX.X, op=ALU.add), reads=["XHS"], writes=["sth"])
                sch.add("dve", lambda e: e.memset(ONESF, 1.0), writes=["ZC"])
                sch.add("pe", lambda e: e.matmul(out=bank(6)[:, 0:2], lhsT=ONESF, rhs=hs_, start=True, stop=True),
                        reads=["sth", "ZC"], writes=["P6"])
                rsqrt_ops(hr_, bank(6)[:, 0:2], 1.0 / D, ["P6"], "sthr")
                sch.add("dve", lambda e: e.tensor_tensor(
                    out=XHS[:], in0=XH[:], in1=col(c.c_gmix, DC).unsqueeze(2).broadcast_to([128, DC, 2]), op=ALU.mult),
                    reads=["XH", "COLS", "XHS"], writes=["XHS"])
                sch.add("dve", lambda e: e.tensor_tensor(
                    out=HTH[:], in0=XHS[:], in1=hr_.unsqueeze(1).broadcast_to([128, DC, 2]), op=ALU.mult),
                    reads=["XHS", "sthr"], writes=["HTH"])

            pend = []

            def flush_pend(n_keep=0):
                while len(pend) > n_keep:
                    pend.pop(0)()

            def stats_op(k2, bankno, colbase, first):
                def f(e):
                    r = None
                    for tc in range(TC):
                        r = e.matmul(out=bank(bankno)[:, colbase + tc:colbase + tc + 1],
                                     lhsT=YSQ[:, k2, tc * 128:(tc + 1) * 128], rhs=ONESB[:, 0:1],
                                     start=(first and tc == 0), stop=False, skip_group_check=True)
                    return r
                sch.add("pe", f, reads=["YSQ%d" % k2, "ONESB"], writes=["P%d" % bankno])

            hook = {"f": None}
            for vj in range(NHA // SC):
                c0 = c.DA + vj * SW
                for q in range(NPQ):
                    si = next_slot("early")
                    s3 = slot3(si, KP, SW)
                    weight_dma(si, [(s3, win_v[:, q * KP:(q + 1) * KP, c0:c0 + SW])])
                    for tc in range(TC):
                        def f(e, q=q, tc=tc, s3=s3):
                            r = None
                            for j in range(KP):
                                r = e.matmul(out=bank(tc)[:, 0:SW], lhsT=HT[:, q * KP + j, tc * 128:(tc + 1) * 128],
                                             rhs=s3[:, j, :], start=(q == 0 and j == 0),
                                             stop=(q == NPQ - 1 and j == KP - 1))
                            return r
                        sch.add("pe", f, reads=[slot_res[si]] + ht_res(q, tc), writes=["P%d" % tc])
                nst = TC * SC
                ci0 = stcol(4 * nst)
                S1, S2 = ST[:, ci0:ci0 + nst], ST[:, ci0 + nst:ci0 + 2 * nst]
                MQ, RS = ST[:, ci0 + 2 * nst:ci0 + 3 * nst], ST[:, ci0 + 3 * nst:ci0 + 4 * nst]
                rn = "sv%d" % ci0
                for tc in range(TC):
                    sch.add("act", lambda e, tc=tc: e.activation(out=TMP[:, tc, :], in_=bank(tc)[:, 0:SW], func=AF.Gelu_apprx_tanh),
                            reads=["P%d" % tc], writes=["T%d" % tc])
                for tc in range(TC):
                    k = tc % 2
                    vg3 = TMP[:, tc, :].rearrange("p (h d) -> p h d", d=128)
                    vsq3 = TMP[:, 4 + k, :].rearrange("p (h d) -> p h d", d=128)
                    sch.add("act", lambda e, tc=tc, k=k: e.activation(out=TMP[:, 4 + k, :], in_=TMP[:, tc, :], func=AF.Square),
                            reads=["T%d" % tc], writes=["T%d" % (4 + k)])
                    sch.add("dve", lambda e, tc=tc, vg3=vg3, S1=S1: e.tensor_reduce(out=S1[:, tc * SC:(tc + 1) * SC], in_=vg3, axis=AX.X, op=ALU.add),
                            reads=["T%d" % tc], writes=[rn + "a"])
                    sch.add("dve", lambda e, tc=tc, vsq3=vsq3, S2=S2: e.tensor_reduce(out=S2[:, tc * SC:(tc + 1) * SC], in_=vsq3, axis=AX.X, op=ALU.add),
                            reads=["T%d" % (4 + k)], writes=[rn + "b"])
                sch.add("dve", lambda e, S1=S1: e.tensor_scalar(out=S1, in0=S1, scalar1=1.0 / 128, scalar2=None, op0=ALU.mult),
                        reads=[rn + "a"], writes=[rn + "a"])
                sch.add("dve", lambda e, S1=S1, MQ=MQ: e.tensor_tensor(out=MQ, in0=S1, in1=S1, op=ALU.mult),
                        reads=[rn + "a"], writes=[rn + "c"])
                sch.add("dve", lambda e, RS=RS, S2=S2, MQ=MQ: e.scalar_tensor_tensor(
                    out=RS, in0=S2, scalar=1.0 / 128, in1=MQ, op0=ALU.mult, op1=ALU.subtract),
                    reads=[rn + "b", rn + "c"], writes=[rn + "d"])
                rsqrt_ops(RS, RS, 1.0, [rn + "d"], rn + "d")
                for tc in range(TC):
                    vg3 = TMP[:, tc, :].rearrange("p (h d) -> p h d", d=128)
                    sch.add("dve", lambda e, vg3=vg3, tc=tc, S1=S1: e.tensor_tensor(
                        out=vg3, in0=vg3, in1=S1[:, tc * SC:(tc + 1) * SC].unsqueeze(2).broadcast_to([128, SC, 128]), op=ALU.subtract),
                        reads=["T%d" % tc, rn + "a"], writes=["T%d" % tc])
                    sch.add("dve", lambda e, vg3=vg3, tc=tc, vj=vj, RS=RS: e.tensor_tensor(
                        out=VLN[:, tc, vj * SW:(vj + 1) * SW].rearrange("p (h d) -> p h d", d=128), in0=vg3,
                        in1=RS[:, tc * SC:(tc + 1) * SC].unsqueeze(2).broadcast_to([128, SC, 128]), op=ALU.mult),
                        reads=["T%d" % tc, rn + "d"], writes=VLN_RES)

            def after_piece1(q, s3, si):
                if q == min(1, NPQ - 1):
                    flush_pend(0)

            for uj in range(NHA // SC):
                c0 = uj * SW
                proj_fm(win_v, [(c0, SW)], SC, "early", after_piece1)
                for ci in range(SC):
                    sch.add("act", lambda e, ci=ci: e.activation(out=TMP[:, ci, :][:, 0:T], in_=bank(ci)[:, 0:T], func=AF.Gelu_apprx_tanh),
                            reads=["P%d" % ci], writes=["T%d" % ci])
                for ci in range(SC):
                    h = uj * SC + ci
                    k = h % 2
                    gu, zz = TMP[:, ci, :][:, 0:T], TMP[:, 4 + k, :][:, 0:T]
                    sb_ = 4 + k
                    def f(e, h=h, sb_=sb_):
                        r = None
                        for tc in range(TC):
                            r = e.matmul(out=bank(sb_)[:, tc * 128:(tc + 1) * 128], lhsT=VLN[:, tc, h * 128:(h + 1) * 128],
                                         rhs=WST[:, h, :], start=True, stop=True)
                        return r
                    sch.add("pe", f, reads=VLN_RES + ["WST"], writes=["P%d" % sb_])
                    zz3 = zz.rearrange("p (tc t) -> p tc t", t=128)
                    sch.add("dve", lambda e, h=h, sb_=sb_, zz3=zz3: e.scalar_tensor_tensor(
                        out=zz3, in0=bank(sb_)[:, 0:T].rearrange("p (tc t) -> p tc t", t=128), scalar=col(c.c_lnvg + h),
                        in1=CB[:, h, :].unsqueeze(1).broadcast_to([128, TC, 128]), op0=ALU.mult, op1=ALU.add),
                        reads=["P%d" % sb_, "CB", "COLS"], writes=["T%d" % (4 + k)])
                    sch.add("dve", lambda e, gu=gu, zz=zz: e.tensor_tensor(out=zz, in0=gu, in1=zz, op=ALU.mult),
                            reads=["T%d" % ci, "T%d" % (4 + k)], writes=["T%d" % (4 + k)])
                    sch.add("act", lambda e, h=h, zz=zz: e.mul(out=YT[:, h, :], in_=zz, mul=col(c.c_ga + h)),
                            reads=["T%d" % (4 + k), "COLS"], writes=[yres(h)])
                    sch.add("act", lambda e, zz=zz, ci=ci: e.activation(out=YSQ[:, ci, :], in_=zz, func=AF.Square),
                            reads=["T%d" % (4 + k)], writes=["YSQ%d" % ci])
                    pend.append(lambda ci=ci, first=(h == 0): stats_op(ci, 7, 0, first))

            hb = SC // 2
            nbj = NCB // SC
            nb_early = nbj // 2
            first_cx = True
            for bj in range(nbj):
                bmode = "early" if bj < nb_early else "late"
                if bj == nb_early:
                    for tc in range(TC):
                        sch.add("sp", lambda e, tc=tc, r0=r0: [e.dma_start(
                            out=XRES[:, tc, :], in_=x_d[r0 + tc * 128:r0 + (tc + 1) * 128, :])],
                            writes=xr(tc), dma_key="x%d" % tc)
                for half in range(2):
                    cxk = bj * 2 + half
                    cC = 2 * c.DA + c.DB + cxk * hb * 128
                    cX = 2 * c.DA + 2 * c.DB + cxk * hb * 128

                    def extra(q, s3, si, cxk=cxk):
                        after_piece1(q, s3, si)
                        if ti != 0:
                            return
                        def f(e):
                            r = None
                            for ci in range(SC):
                                for j in range(KP):
                                    r = e.matmul(out=bank(6)[:, 2 * ci:2 * ci + 2], lhsT=s3[:, j, ci * 128:(ci + 1) * 128],
                                                 rhs=HTH[:, q * KP + j, :],
                                                 start=(q == 0 and ci == 0 and j == 0), stop=False,
                                                 skip_group_check=True)
                            return r
                        sch.add("pe", f, reads=[slot_res[si], "HTH"], writes=["P6"])
                    proj_fm(win_v, [(cC, hb * 128), (cX, hb * 128)], SC, bmode, extra)
                    if first_cx:
                        first_cx = False
                        cA = stcol(TC)
                        RA = ST[:, cA:cA + TC]
                        rsqrt_ops(RA, bank(7)[:, 0:TC], 1.0 / c.DA, ["P7"], "RA")
                    for i in range(hb):
                        sch.add("act", lambda e, i=i: e.copy(out=TMP[:, i, :][:, 0:T], in_=bank(i)[:, 0:T]),
                                reads=["P%d" % i], writes=["T%d" % i])
                    if ti == 0:
                        sch.add("act", lambda e: e.copy(out=CH[:, 0:hb, :], in_=bank(6)[:, 0:2 * hb].rearrange("p (a b) -> p a b", b=2)),
                                reads=["P6"], writes=["CH"])
                    for i in range(hb):
                        zi = half * hb + i
                        cglob = cxk * hb + i
                        ct = TMP[:, i, :][:, 0:T]
                        sch.add("dve", lambda e, i=i, zi=zi, ct=ct: e.tensor_tensor(
                            out=ZC[:, zi, 2:T + 2], in0=bank(hb + i)[:, 0:T], in1=ct, op=ALU.mult),
                            reads=["P%d" % (hb + i), "T%d" % i], writes=["ZC"])
                        if ti == 0:
                            sch.add("dve", lambda e, i=i, zi=zi: e.tensor_tensor(
                                out=ZC[:, zi, 0:2], in0=bank(6)[:, 2 * (hb + i):2 * (hb + i) + 2], in1=CH[:, i, :], op=ALU.mult),
                                reads=["P6", "CH"], writes=["ZC"])
                        else:
                            sch.add("dve", lambda e, zi=zi, cglob=cglob: e.tensor_copy(out=ZC[:, zi, 0:2], in_=ZSAVE[:, cglob, :]),
                                    reads=["ZSAVE"], writes=["ZC"])
                        sch.add("dve", lambda e, zi=zi, cglob=cglob: e.tensor_copy(out=ZSAVE[:, cglob, :], in_=ZC[:, zi, T:T + 2]),
                                reads=["ZC", "ZSAVE"], writes=["ZSAVE"])
                for ci in range(SC):
                    cb = bj * SC + ci
                    t1 = TMP[:, 2 + ci, :][:, 0:T]
                    w = lambda kk, cb=cb: col(c.c_cw + kk * NCB + cb)
                    sch.add("act", lambda e, ci=ci, t1=t1, w=w: e.mul(out=t1, in_=ZC[:, ci, 0:T], mul=w(0)),
                            reads=["ZC", "COLS"], writes=["T%d" % (2 + ci)])
                    sch.add("dve", lambda e, ci=ci, t1=t1, w=w: e.scalar_tensor_tensor(
                        out=t1, in0=ZC[:, ci, 1:T + 1], scalar=w(1), in1=t1, op0=ALU.mult, op1=ALU.add),
                        reads=["ZC", "COLS", "T%d" % (2 + ci)], writes=["T%d" % (2 + ci)])
                    sch.add("dve", lambda e, ci=ci, t1=t1, w=w: e.scalar_tensor_tensor(
                        out=t1, in0=ZC[:, ci, 2:T + 2], scalar=w(2), in1=t1, op0=ALU.mult, op1=ALU.add),
                        reads=["ZC", "COLS", "T%d" % (2 + ci)], writes=["T%d" % (2 + ci)])
                cB = 2 * c.DA + bj * SW
                proj_fm(win_v, [(cB, SW)], SC, bmode, after_piece1)
                for ci in range(SC):
                    t1 = TMP[:, 2 + ci, :][:, 0:T]
                    sch.add("dve", lambda e, ci=ci, t1=t1: e.tensor_tensor(out=t1, in0=bank(ci)[:, 0:T], in1=t1, op=ALU.mult),
                            reads=["P%d" % ci, "T%d" % (2 + ci)], writes=["T%d" % (2 + ci)])
                for ci in range(SC):
                    cb = bj * SC + ci
                    t1 = TMP[:, 2 + ci, :][:, 0:T]
                    sch.add("act", lambda e, cb=cb, t1=t1: e.mul(out=YT[:, NHA + cb, :], in_=t1, mul=col(c.c_gb + cb)),
                            reads=["T%d" % (2 + ci), "COLS"], writes=[yres(NHA + cb)])
                    sch.add("act", lambda e, t1=t1, ci=ci: e.activation(out=YSQ[:, ci, :], in_=t1, func=AF.Square),
                            reads=["T%d" % (2 + ci)], writes=["YSQ%d" % ci])
                    pend.append(lambda ci=ci, first=(cb == 0): stats_op(ci, 4, 0, first))

            cBc = stcol(TC)
            RB = ST[:, cBc:cBc + TC]
            rb_done = False
            for ds in range(c.DS):
                d0 = ds * c.DSW
                for part, (cc0, ncc, RR, rname) in enumerate(((0, NHA, RA, "RA"), (NHA, NCB, RB, "RB"))):
                    npq = ncc // KP
                    for q in range(npq):
                        si = next_slot("late")
                        s3 = slot3(si, KP, c.DSW)
                        weight_dma(si, [(s3, wout_v[:, cc0 + q * KP:cc0 + (q + 1) * KP, d0:d0 + c.DSW])])
                        for tc in range(TC):
                            def f(e, q=q, tc=tc, s3=s3, cc0=cc0, npq=npq):
                                r = None
                                for j in range(KP):
                                    r = e.matmul(out=bank(tc)[:, 0:c.DSW], lhsT=YT[:, cc0 + q * KP + j, tc * 128:(tc + 1) * 128],
                                                 rhs=s3[:, j, :], start=(q == 0 and j == 0),
                                                 stop=(q == npq - 1 and j == KP - 1))
                                return r
                            yr = sorted(set(yres(cc0 + q * KP + j) for j in range(KP)))
                            sch.add("pe", f, reads=[slot_res[si]] + yr, writes=["P%d" % tc])
                        if not rb_done:
                            rb_done = True
                            flush_pend(0)
                            rsqrt_ops(RB, bank(4)[:, 0:TC], 1.0 / c.DB, ["P4"], "RB")
                    for tc in range(TC):
                        sch.add("dve", lambda e, tc=tc, RR=RR, d0=d0: e.scalar_tensor_tensor(
                            out=XRES[:, tc, d0:d0 + c.DSW], in0=bank(tc)[:, 0:c.DSW], scalar=RR[:, tc:tc + 1],
                            in1=XRES[:, tc, d0:d0 + c.DSW], op0=ALU.mult, op1=ALU.add),
                            reads=["P%d" % tc, rname] + xr(tc), writes=xr(tc))

            emit_norm(ti, 1, "ffn")

            NG = (FC + G - 1) // G
            dbank = {"i": 0}
            deferred_down = []

            def emit_down(gj, nchg, k):
                for ds in range(c.DS):
                    d0 = ds * c.DSW
                    si = next_slot("late")
                    s3 = slot3(si, nchg, c.DSW)
                    weight_dma(si, [(s3, wd_v[:, gj * G:gj * G + nchg, d0:d0 + c.DSW])])
                    for tc in range(TC):
                        b = 4 + (dbank["i"] % 4); dbank["i"] += 1
                        def f(e, tc=tc, s3=s3, b=b, nchg=nchg, k=k):
                            r = None
                            for gi in range(nchg):
                                r = e.matmul(out=bank(b)[:, 0:c.DSW], lhsT=AT[k][:, gi, tc * 128:(tc + 1) * 128],
                                             rhs=s3[:, gi, :], start=(gi == 0), stop=(gi == nchg - 1))
                            return r
                        sch.add("pe", f, reads=[slot_res[si], ATRES[k]], writes=["P%d" % b])
                        sch.add("dve", lambda e, tc=tc, b=b, d0=d0: e.tensor_tensor(
                            out=XRES[:, tc, d0:d0 + c.DSW], in0=bank(b)[:, 0:c.DSW], in1=XRES[:, tc, d0:d0 + c.DSW], op=ALU.add),
                            reads=["P%d" % b] + xr(tc), writes=xr(tc))

            for gj in range(NG):
                f0 = gj * G
                nchg = min(G, FC - f0)
                k = gj % 2
                for s0 in range(0, nchg, SC):
                    nch = min(SC, nchg - s0)
                    cF = (f0 + s0) * 128
                    proj_fm(wg_v, [(cF, nch * 128)], nch, "late")
                    for ci in range(nch):
                        sch.add("act", lambda e, ci=ci: e.activation(out=SG[:, ci, :], in_=bank(ci)[:, 0:T], func=AF.Silu),
                                reads=["P%d" % ci], writes=["SG%d" % ci])
                    proj_fm(wu_v, [(cF, nch * 128)], nch, "late")
                    for ci in range(nch):
                        sch.add("dve", lambda e, ci=ci, k=k, s0=s0: e.tensor_tensor(
                            out=AT[k][:, s0 + ci, :], in0=bank(ci)[:, 0:T], in1=SG[:, ci, :], op=ALU.mult),
                            reads=["P%d" % ci, "SG%d" % ci], writes=[ATRES[k]])
                    if s0 == 0 and deferred_down:
                        deferred_down.pop(0)()
                deferred_down.append(lambda gj=gj, nchg=nchg, k=k: emit_down(gj, nchg, k))
                if gj == 0:
                    pass
            while deferred_down:
                deferred_down.pop(0)()

            sch.add("sp", lambda e: [e.dma_start(out=GB, in_=gains_d[2].partition_broadcast(128))],
                    writes=["YC"], dma_key="gb")
            finfo = {}
            allht = [r for q in range(NPQ) for r in ht_res(q)]
            for tc in range(TC):
                ci = stcol(2)
                ss, rr = ST[:, ci:ci + 1], ST[:, ci + 1:ci + 2]
                sres = "sf%d" % ci
                finfo[tc] = (rr, sres)
                sch.add("act", lambda e, tc=tc, ss=ss: e.activation(out=HTflat[:, 0:D], in_=XRES[:, tc, :], func=AF.Square,
                                                                    accum_out=ss),
                        reads=xr(tc), writes=allht + [sres])
                rsqrt_ops(rr, ss, 1.0 / D, [sres], sres + "r")
            for tc in range(TC):
                rr, sres = finfo[tc]
                sch.add("dve", lambda e, tc=tc, rr=rr: e.scalar_tensor_tensor(
                    out=XRES[:, tc, :], in0=XRES[:, tc, :], scalar=rr, in1=GB, op0=ALU.mult, op1=ALU.mult),
                    reads=xr(tc) + [sres + "r", "YC"], writes=xr(tc))
                sch.add("sp", lambda e, tc=tc, r0=r0: [e.dma_start(out=out_d[r0 + tc * 128:r0 + (tc + 1) * 128, :], in_=XRES[:, tc, :])],
                        reads=xr(tc), dma_key="o%d" % tc)

        sems = {}
        for n in sch.sem_names():
            sems[n] = es.enter_context(nc.semaphore(n.replace(":", "_")))
        with nc.allow_low_precision("bf16 matmul operands, fp32 PSUM accumulation"):
            with nc.Block() as block:
                @block.tensor
                def _(e):
                    sch.emit("pe", e, sems)

                @block.scalar
                def _(e):
                    sch.emit("act", e, sems)

                @block.vector
                def _(e):
                    sch.emit("dve", e, sems)

                @block.gpsimd
                def _(e):
                    sch.emit("pool", e, sems)

                @block.sync
                def _(e):
                    sch.emit("sp", e, sems)
                    for tc in range(TC):
                        e.wait_ge(sems["d:o%d" % tc], 16 * sch.dcnt["o%d" % tc])
    return nc


def make_in_maps(cfg, x2d, p):
    c = cfg
    rows = c.NTILE * c.T
    cols = np.zeros((128, c.NCOLS), np.float32)
    cols[:, c.c_lnvg:c.c_lnvg + c.NHA] = p["ln_v_g"].reshape(c.NHA, 128).T
    cols[:, c.c_lnvb:c.c_lnvb + c.NHA] = p["ln_v_b"].reshape(c.NHA, 128).T
    cols[:, c.c_ga:c.c_ga + c.NHA] = p["out_norm_a_g"].reshape(c.NHA, 128).T
    cols[:, c.c_gb:c.c_gb + c.NCB] = p["out_norm_b_g"].reshape(c.NCB, 128).T
    cols[:, c.c_cw:c.c_cw + 3 * c.NCB] = p["conv_w"].reshape(3, c.NCB, 128).transpose(2, 0, 1).reshape(128, 3 * c.NCB)
    cols[:, c.c_gmix:c.c_gmix + c.DC] = p["mix_norm_g"].reshape(c.DC, 128).T
    gains = np.ascontiguousarray(np.stack([p["mix_norm_g"], p["ffn_norm_g"], p["final_norm_g"]]).astype(np.float32))
    shared = {
        "cols": cols, "gains": gains,
        "w_spatial": np.ascontiguousarray(p["w_spatial"], dtype=np.float32),
        "b_spatial": np.ascontiguousarray(p["b_spatial"].reshape(1, -1), dtype=np.float32),
        "w_in": np.ascontiguousarray(p["w_in"], dtype=np.float32),
        "w_out": np.ascontiguousarray(p["w_out"], dtype=np.float32),
        "w_gate": np.ascontiguousarray(p["w_gate"], dtype=np.float32),
        "w_up": np.ascontiguousarray(p["w_up"], dtype=np.float32),
        "w_down": np.ascontiguousarray(p["w_down"], dtype=np.float32),
    }
    in_maps = []
    for i in range(c.NCORES):
        xs = np.ascontiguousarray(x2d[i * rows:(i + 1) * rows])
        halo = np.zeros((2, c.D), np.float32)
        if i > 0:
            halo[:] = x2d[i * rows - 2:i * rows]
        xh = np.ascontiguousarray(halo.reshape(2, c.DC, 128).transpose(2, 1, 0))
        m = dict(shared)
        m["x"] = xs
        m["xh"] = xh
        in_maps.append(m)
    return in_maps


_PROGRAM_CACHE = {}


def kernel(x, mix_norm_g, w_in, ln_v_g, ln_v_b, w_spatial, b_spatial, conv_w,
           out_norm_a_g, out_norm_b_g, w_out, ffn_norm_g, w_gate, w_up, w_down, final_norm_g):
    cfg = Cfg()
    x = np.asarray(x, dtype=np.float32)
    B, S, D = x.shape
    assert D == cfg.D and B * S == cfg.NCORES * cfg.NTILE * cfg.T
    p = {
        "mix_norm_g": np.asarray(mix_norm_g)[0], "w_in": np.asarray(w_in)[0],
        "ln_v_g": np.asarray(ln_v_g)[0], "ln_v_b": np.asarray(ln_v_b)[0],
        "w_spatial": np.asarray(w_spatial)[0], "b_spatial": np.asarray(b_spatial)[0],
        "conv_w": np.asarray(conv_w)[0], "out_norm_a_g": np.asarray(out_norm_a_g)[0],
        "out_norm_b_g": np.asarray(out_norm_b_g)[0], "w_out": np.asarray(w_out)[0],
        "ffn_norm_g": np.asarray(ffn_norm_g)[0], "w_gate": np.asarray(w_gate)[0],
        "w_up": np.asarray(w_up)[0], "w_down": np.asarray(w_down)[0],
        "final_norm_g": np.asarray(final_norm_g),
    }
    in_maps = make_in_maps(cfg, x.reshape(B * S, D), p)
    if "nc" not in _PROGRAM_CACHE:
        _PROGRAM_CACHE["nc"] = build_program(cfg)
    nc = _PROGRAM_CACHE["nc"]
    res = run_bass_kernel_spmd(nc, in_maps, core_ids=list(range(cfg.NCORES)))
    out = np.concatenate([np.asarray(r["out"]) for r in res.results], axis=0)
    return out.reshape(B, S, D).astype(np.float32)
```

```python
import numpy as np
from contextlib import ExitStack

import concourse.bass as bass
import concourse.mybir as mybir
from concourse.bass_utils import run_bass_kernel_spmd

F32 = mybir.dt.float32
BF16 = mybir.dt.bfloat16
AF = mybir.ActivationFunctionType
ALU = mybir.AluOpType
AX = mybir.AxisListType
EPS = 1e-6


class Cfg:
    def __init__(self, D=4096, NHA=16, NCB=16, F=11008, T=512, NTILE=2, SC=4, KP=8, R=3,
                 NCORES=8):
        self.D, self.NHA, self.NCB, self.F, self.T, self.NTILE = D, NHA, NCB, F, T, NTILE
        self.SC, self.KP, self.R, self.NCORES = SC, KP, R, NCORES
        self.DC = D // 128
        self.TC = T // 128
        self.FC = F // 128
        self.DA, self.DB = NHA * 128, NCB * 128
        self.MIXC = NHA + NCB
        self.MIX = self.MIXC * 128
        self.INC = 2 * self.DA + 3 * self.DB
        self.SW = SC * 128
        self.NPQ = self.DC // KP
        self.G = self.MIXC // 4
        self.DS = D // 512 if D >= 512 else 1
        self.DSW = min(512, D)
        assert self.MIXC * T == 4 * D, "YT aliasing (hn0|hn1|GB) needs MIXC*T == 4*D"
        assert self.DC % KP == 0 and NHA % KP == 0 and NCB % KP == 0
        assert NHA % SC == 0 and NCB % SC == 0 and SC % 2 == 0
        assert self.G <= KP and self.G % SC == 0
        assert self.TC * self.DA == 2 * KP * self.SW or True
        o = 0
        self.c_lnvg = o; o += NHA
        self.c_lnvb = o; o += NHA
        self.c_ga = o; o += NHA
        self.c_gb = o; o += NCB
        self.c_cw = o; o += 3 * NCB
        self.c_gmix = o; o += self.DC
        self.NCOLS = o


class _Op:
    __slots__ = ("eng", "emit", "deps", "raw", "is_dma", "key", "done", "n_dma")


class Sched:
    ENGS = ("pe", "act", "dve", "pool", "sp")

    def __init__(self):
        self.ops = []
        self.lastw = {}
        self.readers = {}
        self.cnt = {e: 0 for e in self.ENGS}
        self.dcnt = {}

    def add(self, eng, emit, reads=(), writes=(), dma_key=None, n_dma=1):
        idx = len(self.ops)
        deps, raw = set(), set()
        for r in reads:
            if r in self.lastw:
                deps.add(self.lastw[r]); raw.add(self.lastw[r])
            if r[0] == "P":
                for rd in self.readers.get(r, ()):
                    if self.ops[rd].eng != eng:
                        deps.add(rd)
        for w in writes:
            if w in self.lastw:
                deps.add(self.lastw[w]); raw.add(self.lastw[w])
            deps.update(self.readers.get(w, ()))
        op = _Op()
        op.eng, op.emit, op.is_dma, op.key, op.n_dma = eng, emit, dma_key is not None, dma_key, n_dma
        keep = set()
        for d in deps:
            dop = self.ops[d]
            if dop.is_dma:
                keep.add(d)
            elif dop.eng == eng:
                if eng != "pe" and d in raw:
                    keep.add(d)
            else:
                keep.add(d)
        op.deps = keep
        if op.is_dma:
            self.dcnt[dma_key] = self.dcnt.get(dma_key, 0) + n_dma
            op.done = ("d:" + dma_key, 16 * self.dcnt[dma_key])
        else:
            self.cnt[eng] += 1
            op.done = ("e:" + eng, self.cnt[eng])
        for r in reads:
            self.readers.setdefault(r, []).append(idx)
        for w in writes:
            self.lastw[w] = idx
            self.readers[w] = []
        self.ops.append(op)
        return idx

    def sem_names(self):
        names = ["e:" + e for e in self.ENGS]
        names += ["d:" + k for k in self.dcnt]
        return names

    def emit(self, eng_name, eng, sems):
        known = {}
        for op in self.ops:
            if op.eng != eng_name:
                continue
            waits = {}
            for d in op.deps:
                s, v = self.ops[d].done
                if v > waits.get(s, 0):
                    waits[s] = v
            for s, v in waits.items():
                if known.get(s, 0) >= v:
                    continue
                eng.wait_ge(sems[s], v)
                known[s] = v
            res = op.emit(eng)
            if op.is_dma:
                assert len(res) == op.n_dma
                for ins in res:
                    ins.then_inc(sems[op.done[0]], 16)
            else:
                res.then_inc(sems[op.done[0]], 1)


def build_program(cfg):
    c = cfg
    D, T, TC, DC, SC, KP, SW, NPQ = c.D, c.T, c.TC, c.DC, c.SC, c.KP, c.SW, c.NPQ
    NHA, NCB, MIXC, G, FC = c.NHA, c.NCB, c.MIXC, c.G, c.FC
    ROWS = c.NTILE * T

    nc = bass.Bass("TRN2", target_bir_lowering=False)
    x_d = nc.dram_tensor("x", [ROWS, D], F32, kind="ExternalInput").ap()
    xh_d = nc.dram_tensor("xh", [128, DC, 2], F32, kind="ExternalInput").ap()
    cols_d = nc.dram_tensor("cols", [128, c.NCOLS], F32, kind="ExternalInput").ap()
    gains_d = nc.dram_tensor("gains", [3, D], F32, kind="ExternalInput").ap()
    wsp_d = nc.dram_tensor("w_spatial", [NHA, 128, 128], F32, kind="ExternalInput").ap()
    bsp_d = nc.dram_tensor("b_spatial", [1, NHA * 128], F32, kind="ExternalInput").ap()
    win_d = nc.dram_tensor("w_in", [D, c.INC], F32, kind="ExternalInput").ap()
    wout_d = nc.dram_tensor("w_out", [c.MIX, D], F32, kind="ExternalInput").ap()
    wg_d = nc.dram_tensor("w_gate", [D, c.F], F32, kind="ExternalInput").ap()
    wu_d = nc.dram_tensor("w_up", [D, c.F], F32, kind="ExternalInput").ap()
    wd_d = nc.dram_tensor("w_down", [c.F, D], F32, kind="ExternalInput").ap()
    out_d = nc.dram_tensor("out", [ROWS, D], F32, kind="ExternalOutput").ap()

    win_v = win_d.rearrange("(dc p) c -> p dc c", p=128)
    wout_v = wout_d.rearrange("(cc p) d -> p cc d", p=128)
    wg_v = wg_d.rearrange("(dc p) f -> p dc f", p=128)
    wu_v = wu_d.rearrange("(dc p) f -> p dc f", p=128)
    wd_v = wd_d.rearrange("(fc p) d -> p fc d", p=128)

    es = ExitStack()
    with es:
        def sb(name, shape, dt):
            return es.enter_context(nc.sbuf_tensor(name, shape, dt))

        XRES = sb("XRES", [128, TC, D], F32)
        HT = sb("HT", [128, DC, T], BF16)
        YTf = sb("YT", [128, MIXC * T], BF16)
        VLNf = sb("VLN", [128, TC * c.DA], BF16)
        SLOT = KP * max(SW, c.DSW)
        RING = sb("RING", [128, c.R, SLOT], BF16)
        TMP = sb("TMP", [128, 6, SW], F32)
        ZC = sb("ZC", [128, SC, T + 2], F32)
        YSQ = sb("YSQ", [128, SC, T], BF16)
        IDENT = sb("IDENT", [128, 128], BF16)
        ONESB = sb("ONESB", [128, 128], BF16)
        WST = sb("WST", [128, NHA, 128], BF16)
        CB = sb("CB", [128, NHA, 128], F32)
        COLS = sb("COLS", [128, c.NCOLS], F32)
        ST = sb("ST", [128, 320], F32)
        XH = sb("XH", [128, DC, 2], F32)
        XHS = sb("XHS", [128, DC, 2], F32)
        HTH = sb("HTH", [128, DC, 2], BF16)
        ZSAVE = sb("ZSAVE", [128, NCB, 2], F32)
        CH = sb("CH", [128, SC, 2], F32)
        EPSC = sb("EPSC", [128, 1], F32)
        PS = es.enter_context(nc.psum_tensor("PS", [128, 8, 512], F32))

        YT = YTf[:].rearrange("p (c t) -> p c t", t=T)
        HN = [YTf[:, 0:D], YTf[:, D:2 * D]]
        GB = YTf[:, 2 * D:4 * D].bitcast(F32)
        AT = [YTf[:, 0:G * T].rearrange("p (g t) -> p g t", t=T),
              YTf[:, G * T:2 * G * T].rearrange("p (g t) -> p g t", t=T)]
        VLN = VLNf[:].rearrange("p (tc d) -> p tc d", d=c.DA)
        SG = ZC[:, :, 0:T]
        IDF = HT[:].rearrange("p dc t -> p (dc t)")[:, 0:256].bitcast(F32)
        ONESF = ZC[:].rearrange("p a b -> p (a b)")[:, 0:128]
        HTflat = HT[:].rearrange("p dc t -> p (dc t)")

        def yres(ch):
            q4 = MIXC // 4
            return "YA" if ch < q4 else ("YB" if ch < 2 * q4 else "YC")

        HNRES = ["YA", "YB"]
        ATRES = ["YA", "YB"]

        slot_elems = SLOT
        n_extra = (TC * c.DA) // slot_elems
        n_extra = min(n_extra, 2)
        slots = [RING[:, i, :] for i in range(c.R)] + \
                [VLNf[:, i * slot_elems:(i + 1) * slot_elems] for i in range(n_extra)]
        slot_res = ["S%d" % i for i in range(c.R)] + ["V%d" % i for i in range(n_extra)]
        VLN_RES = ["V%d" % i for i in range(n_extra)] + ["VLNrest"]

        def bank(b):
            return PS[:, b, :]

        def bank_bf(b):
            return PS[:, b, :].bitcast(BF16).rearrange("p (j c) -> p j c", c=128)

        XRf = XRES[:].rearrange("p tc d -> p (tc d)")
        spt = max(1, (D * 2) // slot_elems)
        xs_f32 = (D // spt)
        assert xs_f32 * 2 >= slot_elems
        n_x = TC * spt
        x_slot_ids = list(range(len(slots), len(slots) + n_x))
        for j in range(n_x):
            slots.append(XRf[:, j * xs_f32:j * xs_f32 + slot_elems // 2].bitcast(BF16))
            slot_res.append("XS%d" % j)
        v_slot_ids = list(range(c.R, c.R + n_extra))
        base_ids = list(range(c.R))
        MODES = {"early": base_ids + x_slot_ids, "late": base_ids + v_slot_ids, "base": base_ids}

        def xr(tc):
            return ["XS%d" % (tc * spt + j) for j in range(spt)]

        sch = Sched()
        ring_state = {"seq": 0}
        last_use = {}

        def touch(slot_id):
            ring_state["seq"] += 1
            last_use[slot_id] = ring_state["seq"]

        def next_slot(mode):
            i = min(MODES[mode], key=lambda sid: (last_use.get(sid, -1), sid))
            touch(i)
            return i

        col = lambda a, n=1: COLS[:, a:a + n]

        sch.add("sp", lambda e: [e.dma_start(out=GB, in_=gains_d[0].partition_broadcast(128))],
                writes=["YC"], dma_key="gb")
        sch.add("sp", lambda e: [e.dma_start(out=XRES[:, 0, :], in_=x_d[0:128, :])], writes=xr(0), dma_key="x0")
        sch.add("sp", lambda e: [e.dma_start(out=COLS[:], in_=cols_d)], writes=["COLS"], dma_key="cols")
        WS32 = TMP[:, 0:4, :].rearrange("p a b -> p (a b)")[:, 0:NHA * 128].rearrange("p (h s) -> p h s", s=128) \
            if NHA * 128 <= 4 * SW else None
        assert WS32 is not None
        BSB = ZC[:].rearrange("p a b -> p (a b)")[:, 0:NHA * 128].rearrange("p (h t) -> p h t", t=128)
        assert NHA * 128 <= SC * (T + 2)
        sch.add("sp", lambda e: [e.dma_start(out=WS32, in_=wsp_d.rearrange("h t s -> t h s"))],
                writes=["T0", "T1", "T2", "T3"], dma_key="wsp")
        sch.add("sp", lambda e: [e.dma_start(out=BSB, in_=bsp_d[0].partition_broadcast(128).rearrange("p (h t) -> p h t", t=128))],
                writes=["ZC"], dma_key="bsp")
        sch.add("pool", lambda e: e.memset(IDF, 0.0), writes=["IDF"])
        sch.add("pool", lambda e: e.affine_select(out=IDF, in_=IDF, pattern=[[-1, 128]],
                                                  compare_op=ALU.not_equal, fill=1.0, base=0,
                                                  channel_multiplier=1), reads=["IDF"], writes=["IDF"])
        sch.add("dve", lambda e: e.tensor_copy(out=IDENT[:], in_=IDF), reads=["IDF"], writes=["IDENT"])
        sch.add("dve", lambda e: e.memset(ONESB[:], 1.0), writes=["ONESB"])
        sch.add("dve", lambda e: e.memset(ZSAVE[:], 0.0), writes=["ZSAVE"])
        sch.add("dve", lambda e: e.memset(EPSC[:], EPS), writes=["EPSC"])
        sch.add("pool", lambda e: e.affine_select(out=WS32, in_=WS32, pattern=[[0, NHA], [-1, 128]],
                                                  compare_op=ALU.is_ge, fill=0.0, base=0,
                                                  channel_multiplier=1),
                reads=["T0", "T1", "T2", "T3"], writes=["T0", "T1", "T2", "T3"])
        WSB = TMP[:, 4:6, :].rearrange("p a b -> p (a b)").bitcast(BF16)[:, 0:NHA * 128].rearrange("p (h s) -> p h s", s=128)
        sch.add("dve", lambda e: e.tensor_copy(out=WSB, in_=WS32), reads=["T0", "T1", "T2", "T3"], writes=["T4", "T5"])
        for h0 in range(0, NHA, 8):
            hs = list(range(h0, min(h0 + 8, NHA)))
            def f(e, hs=hs):
                r = None
                for j, h in enumerate(hs):
                    r = e.transpose(out=bank_bf(6)[:, j, :], in_=WSB[:, h, :], identity=IDENT[:])
                return r
            sch.add("pe", f, reads=["T4", "T5", "IDENT"], writes=["P6"])
            sch.add("act", lambda e, hs=hs: e.copy(out=WST[:, hs[0]:hs[-1] + 1, :], in_=bank_bf(6)[:, 0:len(hs), :]),
                    reads=["P6"], writes=["WST"])
        for h0 in range(0, NHA, 4):
            hs = list(range(h0, min(h0 + 4, NHA)))
            def f(e, hs=hs):
                r = None
                for j, h in enumerate(hs):
                    r = e.matmul(out=bank(7)[:, j * 128:(j + 1) * 128], lhsT=ONESB[:], rhs=WST[:, h, :],
                                 start=True, stop=True)
                return r
            sch.add("pe", f, reads=["WST", "ONESB"], writes=["P7"])
            for j, h in enumerate(hs):
                sch.add("dve", lambda e, j=j, h=h: e.scalar_tensor_tensor(
                    out=CB[:, h, :], in0=bank(7)[:, j * 128:(j + 1) * 128], scalar=col(c.c_lnvb + h),
                    in1=BSB[:, h, :], op0=ALU.mult, op1=ALU.add),
                    reads=["P7", "ZC", "COLS"], writes=["CB"])

        stc = {"i": 0}

        def stcol(n=1):
            i = stc["i"]
            if i + n > 320:
                i = 0
            stc["i"] = i + n
            return i

        def rsqrt_ops(dst, src, scale, rd, wr):
            sch.add("act", lambda e: e.activation(out=dst, in_=src, func=AF.Sqrt, bias=EPSC[:, 0:1], scale=scale),
                    reads=list(rd) + ["EPSC"], writes=[wr])
            sch.add("dve", lambda e: e.reciprocal(out=dst, in_=dst), reads=[wr], writes=[wr])

        def emit_norm(ti, gain_row, phase, load_gain=True):
            if load_gain:
                sch.add("sp", lambda e: [e.dma_start(out=GB, in_=gains_d[gain_row].partition_broadcast(128))],
                        writes=["YC"], dma_key="gb")
            JUNK = VLNf[:, 0:D]
            info = {}

            def s1(tc):
                ci = stcol(2)
                ss, rr = ST[:, ci:ci + 1], ST[:, ci + 1:ci + 2]
                sres = "st%d" % ci
                info[tc] = (rr, sres)
                sch.add("act", lambda e, tc=tc, ss=ss: e.activation(
                    out=JUNK, in_=XRES[:, tc, :], func=AF.Square, accum_out=ss),
                    reads=xr(tc), writes=VLN_RES + [sres])
                rsqrt_ops(rr, ss, 1.0 / D, [sres], sres + "r")

            def s2(tc):
                k = tc % 2
                rr, sres = info[tc]
                sch.add("dve", lambda e, tc=tc, k=k, rr=rr: e.scalar_tensor_tensor(
                    out=HN[k], in0=XRES[:, tc, :], scalar=rr, in1=GB, op0=ALU.mult, op1=ALU.mult),
                    reads=xr(tc) + [sres + "r", "YC"], writes=[HNRES[k]])

            def s3(tc):
                k = tc % 2
                for q in range(NPQ):
                    pb = 6 + (q % 2)
                    def f(e, q=q, k=k, pb=pb):
                        r = None
                        for j in range(KP):
                            dc = q * KP + j
                            r = e.transpose(out=bank_bf(pb)[:, j, :], in_=HN[k][:, dc * 128:(dc + 1) * 128],
                                            identity=IDENT[:])
                        return r
                    sch.add("pe", f, reads=[HNRES[k], "IDENT"], writes=["P%d" % pb])
                    eng = "act" if (q % 2 == 0) else "dve"
                    def g(e, q=q, tc=tc, pb=pb, eng=eng):
                        o = HT[:, q * KP:(q + 1) * KP, tc * 128:(tc + 1) * 128]
                        i = bank_bf(pb)[:, 0:KP, :]
                        return e.copy(out=o, in_=i) if eng == "act" else e.tensor_copy(out=o, in_=i)
                    sch.add(eng, g, reads=["P%d" % pb], writes=["HT%d_%d" % (q, tc)])

            s1(0)
            for tc in range(TC):
                if tc + 1 < TC:
                    s1(tc + 1)
                s2(tc)
                if tc >= 1:
                    s3(tc - 1)
            s3(TC - 1)

        def ht_res(q, tc=None):
            if tc is None:
                return ["HT%d_%d" % (q, t) for t in range(TC)]
            return ["HT%d_%d" % (q, tc)]

        def weight_dma(slot_i, runs):
            def f(e):
                return [e.dma_start(out=o, in_=i) for (o, i) in runs]
            sch.add("pool", f, writes=[slot_res[slot_i]], dma_key="w%d" % slot_i, n_dma=len(runs))

        def slot3(slot_i, nk, width):
            return slots[slot_i][:, 0:nk * width].rearrange("p (k w) -> p k w", w=width)

        def proj_fm(src_v, col_runs, nch, mode, extra_after_piece=None):
            width = sum(n for (_, n) in col_runs)
            for q in range(NPQ):
                si = next_slot(mode)
                s3 = slot3(si, KP, width)
                runs, off = [], 0
                for (c0, n) in col_runs:
                    runs.append((s3[:, :, off:off + n], src_v[:, q * KP:(q + 1) * KP, c0:c0 + n]))
                    off += n
                weight_dma(si, runs)
                for ci in range(nch):
                    def f(e, q=q, ci=ci, s3=s3):
                        r = None
                        for j in range(KP):
                            r = e.matmul(out=bank(ci)[:, 0:T], lhsT=s3[:, j, ci * 128:(ci + 1) * 128],
                                         rhs=HT[:, q * KP + j, :],
                                         start=(q == 0 and j == 0), stop=(q == NPQ - 1 and j == KP - 1))
                        return r
                    sch.add("pe", f, reads=[slot_res[si]] + ht_res(q), writes=["P%d" % ci])
                if extra_after_piece is not None:
                    extra_after_piece(q, s3, si)

        for ti in range(c.NTILE):
            r0 = ti * T
            stc["i"] = 0
            for sid in x_slot_ids:
                touch(sid)
            if ti != 0:
                sch.add("sp", lambda e: [e.dma_start(out=GB, in_=gains_d[0].partition_broadcast(128))],
                        writes=["YC"], dma_key="gb")
            for tc in range(TC):
                if ti == 0 and tc == 0:
                    continue
                sch.add("sp", lambda e, tc=tc, r0=r0: [e.dma_start(out=XRES[:, tc, :],
                                                                  in_=x_d[r0 + tc * 128:r0 + (tc + 1) * 128, :])],
                        writes=xr(tc), dma_key="x%d" % tc)
            emit_norm(ti, 0, "mix", load_gain=False)

            if ti == 0:
                sch.add("sp", lambda e: [e.dma_start(out=XH[:], in_=xh_d)], writes=["XH"], dma_key="xh")
                sch.add("act", lambda e: e.activation(out=XHS[:], in_=XH[:], func=AF.Square), reads=["XH"], writes=["XHS"])
                hci = stcol(4)
                hs_, hr_ = ST[:, hci:hci + 2], ST[:, hci + 2:hci + 4]
                sch.add("dve", lambda e: e.tensor_reduce(out=hs_, in_=XHS[:].rearrange("p dc j -> p j dc"),
                                                         axis=AX.X, op=ALU.add), reads=["XHS"], writes=["sth"])
                sch.add("dve", lambda e: e.memset(ONESF, 1.0), writes=["ZC"])
                sch.add("pe", lambda e: e.matmul(out=bank(6)[:, 0:2], lhsT=ONESF, rhs=hs_, start=True, stop=True),
                        reads=["sth", "ZC"], writes=["P6"])
                rsqrt_ops(hr_, bank(6)[:, 0:2], 1.0 / D, ["P6"], "sthr")
                sch.add("dve", lambda e: e.tensor_tensor(
                    out=XHS[:], in0=XH[:], in1=col(c.c_gmix, DC).unsqueeze(2).broadcast_to([128, DC, 2]), op=ALU.mult),
                    reads=["XH", "COLS", "XHS"], writes=["XHS"])
                sch.add("dve", lambda e: e.tensor_tensor(
                    out=HTH[:], in0=XHS[:], in1=hr_.unsqueeze(1).broadcast_to([128, DC, 2]), op=ALU.mult),
                    reads=["XHS", "sthr"], writes=["HTH"])

            pend = []

            def flush_pend(n_keep=0):
                while len(pend) > n_keep:
                    pend.pop(0)()

            def stats_op(k2, bankno, colbase, first):
                def f(e):
                    r = None
                    for tc in range(TC):
                        r = e.matmul(out=bank(bankno)[:, colbase + tc:colbase + tc + 1],
                                     lhsT=YSQ[:, k2, tc * 128:(tc + 1) * 128], rhs=ONESB[:, 0:1],
                                     start=(first and tc == 0), stop=False, skip_group_check=True)
                    return r
                sch.add("pe", f, reads=["YSQ%d" % k2, "ONESB"], writes=["P%d" % bankno])

            hook = {"f": None}
            for vj in range(NHA // SC):
                c0 = c.DA + vj * SW
                for q in range(NPQ):
                    si = next_slot("early")
                    s3 = slot3(si, KP, SW)
                    weight_dma(si, [(s3, win_v[:, q * KP:(q + 1) * KP, c0:c0 + SW])])
                    for tc in range(TC):
                        def f(e, q=q, tc=tc, s3=s3):
                            r = None
                            for j in range(KP):
                                r = e.matmul(out=bank(tc)[:, 0:SW], lhsT=HT[:, q * KP + j, tc * 128:(tc + 1) * 128],
                                             rhs=s3[:, j, :], start=(q == 0 and j == 0),
                                             stop=(q == NPQ - 1 and j == KP - 1))
                            return r
                        sch.add("pe", f, reads=[slot_res[si]] + ht_res(q, tc), writes=["P%d" % tc])
                nst = TC * SC
                ci0 = stcol(4 * nst)
                S1, S2 = ST[:, ci0:ci0 + nst], ST[:, ci0 + nst:ci0 + 2 * nst]
                MQ, RS = ST[:, ci0 + 2 * nst:ci0 + 3 * nst], ST[:, ci0 + 3 * nst:ci0 + 4 * nst]
                rn = "sv%d" % ci0
                for tc in range(TC):
                    sch.add("act", lambda e, tc=tc: e.activation(out=TMP[:, tc, :], in_=bank(tc)[:, 0:SW], func=AF.Gelu_apprx_tanh),
                            reads=["P%d" % tc], writes=["T%d" % tc])
                for tc in range(TC):
                    k = tc % 2
                    vg3 = TMP[:, tc, :].rearrange("p (h d) -> p h d", d=128)
                    vsq3 = TMP[:, 4 + k, :].rearrange("p (h d) -> p h d", d=128)
                    sch.add("act", lambda e, tc=tc, k=k: e.activation(out=TMP[:, 4 + k, :], in_=TMP[:, tc, :], func=AF.Square),
                            reads=["T%d" % tc], writes=["T%d" % (4 + k)])
                    sch.add("dve", lambda e, tc=tc, vg3=vg3, S1=S1: e.tensor_reduce(out=S1[:, tc * SC:(tc + 1) * SC], in_=vg3, axis=AX.X, op=ALU.add),
                            reads=["T%d" % tc], writes=[rn + "a"])
                    sch.add("dve", lambda e, tc=tc, vsq3=vsq3, S2=S2: e.tensor_reduce(out=S2[:, tc * SC:(tc + 1) * SC], in_=vsq3, axis=AX.X, op=ALU.add),
                            reads=["T%d" % (4 + k)], writes=[rn + "b"])
                sch.add("dve", lambda e, S1=S1: e.tensor_scalar(out=S1, in0=S1, scalar1=1.0 / 128, scalar2=None, op0=ALU.mult),
                        reads=[rn + "a"], writes=[rn + "a"])
                sch.add("dve", lambda e, S1=S1, MQ=MQ: e.tensor_tensor(out=MQ, in0=S1, in1=S1, op=ALU.mult),
                        reads=[rn + "a"], writes=[rn + "c"])
                sch.add("dve", lambda e, RS=RS, S2=S2, MQ=MQ: e.scalar_tensor_tensor(
                    out=RS, in0=S2, scalar=1.0 / 128, in1=MQ, op0=ALU.mult, op1=ALU.subtract),
                    reads=[rn + "b", rn + "c"], writes=[rn + "d"])
                rsqrt_ops(RS, RS, 1.0, [rn + "d"], rn + "d")
                for tc in range(TC):
                    vg3 = TMP[:, tc, :].rearrange("p (h d) -> p h d", d=128)
                    sch.add("dve", lambda e, vg3=vg3, tc=tc, S1=S1: e.tensor_tensor(
                        out=vg3, in0=vg3, in1=S1[:, tc * SC:(tc + 1) * SC].unsqueeze(2).broadcast_to([128, SC, 128]), op=ALU.subtract),
                        reads=["T%d" % tc, rn + "a"], writes=["T%d" % tc])
                    sch.add("dve", lambda e, vg3=vg3, tc=tc, vj=vj, RS=RS: e.tensor_tensor(
                        out=VLN[:, tc, vj * SW:(vj + 1) * SW].rearrange("p (h d) -> p h d", d=128), in0=vg3,
                        in1=RS[:, tc * SC:(tc + 1) * SC].unsqueeze(2).broadcast_to([128, SC, 128]), op=ALU.mult),
                        reads=["T%d" % tc, rn + "d"], writes=VLN_RES)

            def after_piece1(q, s3, si):
                if q == min(1, NPQ - 1):
                    flush_pend(0)

            for uj in range(NHA // SC):
                c0 = uj * SW
                proj_fm(win_v, [(c0, SW)], SC, "early", after_piece1)
                for ci in range(SC):
                    sch.add("act", lambda e, ci=ci: e.activation(out=TMP[:, ci, :][:, 0:T], in_=bank(ci)[:, 0:T], func=AF.Gelu_apprx_tanh),
                            reads=["P%d" % ci], writes=["T%d" % ci])
                for ci in range(SC):
                    h = uj * SC + ci
                    k = h % 2
                    gu, zz = TMP[:, ci, :][:, 0:T], TMP[:, 4 + k, :][:, 0:T]
                    sb_ = 4 + k
                    def f(e, h=h, sb_=sb_):
                        r = None
                        for tc in range(TC):
                            r = e.matmul(out=bank(sb_)[:, tc * 128:(tc + 1) * 128], lhsT=VLN[:, tc, h * 128:(h + 1) * 128],
                                         rhs=WST[:, h, :], start=True, stop=True)
                        return r
                    sch.add("pe", f, reads=VLN_RES + ["WST"], writes=["P%d" % sb_])
                    zz3 = zz.rearrange("p (tc t) -> p tc t", t=128)
                    sch.add("dve", lambda e, h=h, sb_=sb_, zz3=zz3: e.scalar_tensor_tensor(
                        out=zz3, in0=bank(sb_)[:, 0:T].rearrange("p (tc t) -> p tc t", t=128), scalar=col(c.c_lnvg + h),
                        in1=CB[:, h, :].unsqueeze(1).broadcast_to([128, TC, 128]), op0=ALU.mult, op1=ALU.add),
                        reads=["P%d" % sb_, "CB", "COLS"], writes=["T%d" % (4 + k)])
                    sch.add("dve", lambda e, gu=gu, zz=zz: e.tensor_tensor(out=zz, in0=gu, in1=zz, op=ALU.mult),
                            reads=["T%d" % ci, "T%d" % (4 + k)], writes=["T%d" % (4 + k)])
                    sch.add("act", lambda e, h=h, zz=zz: e.mul(out=YT[:, h, :], in_=zz, mul=col(c.c_ga + h)),
                            reads=["T%d" % (4 + k), "COLS"], writes=[yres(h)])
                    sch.add("act", lambda e, zz=zz, ci=ci: e.activation(out=YSQ[:, ci, :], in_=zz, func=AF.Square),
                            reads=["T%d" % (4 + k)], writes=["YSQ%d" % ci])
                    pend.append(lambda ci=ci, first=(h == 0): stats_op(ci, 7, 0, first))

            hb = SC // 2
            nbj = NCB // SC
            nb_early = nbj // 2
            first_cx = True
            for bj in range(nbj):
                bmode = "early" if bj < nb_early else "late"
                if bj == nb_early:
                    for tc in range(TC):
                        sch.add("sp", lambda e, tc=tc, r0=r0: [e.dma_start(
                            out=XRES[:, tc, :], in_=x_d[r0 + tc * 128:r0 + (tc + 1) * 128, :])],
                            writes=xr(tc), dma_key="x%d" % tc)
                for half in range(2):
                    cxk = bj * 2 + half
                    cC = 2 * c.DA + c.DB + cxk * hb * 128
                    cX = 2 * c.DA + 2 * c.DB + cxk * hb * 128

                    def extra(q, s3, si, cxk=cxk):
                        after_piece1(q, s3, si)
                        if ti != 0:
                            return
                        def f(e):
                            r = None
                            for ci in range(SC):
                                for j in range(KP):
                                    r = e.matmul(out=bank(6)[:, 2 * ci:2 * ci + 2], lhsT=s3[:, j, ci * 128:(ci + 1) * 128],
                                                 rhs=HTH[:, q * KP + j, :],
                                                 start=(q == 0 and ci == 0 and j == 0), stop=False,
                                                 skip_group_check=True)
                            return r
                        sch.add("pe", f, reads=[slot_res[si], "HTH"], writes=["P6"])
                    proj_fm(win_v, [(cC, hb * 128), (cX, hb * 128)], SC, bmode, extra)
                    if first_cx:
                        first_cx = False
                        cA = stcol(TC)
                        RA = ST[:, cA:cA + TC]
                        rsqrt_ops(RA, bank(7)[:, 0:TC], 1.0 / c.DA, ["P7"], "RA")
                    for i in range(hb):
                        sch.add("act", lambda e, i=i: e.copy(out=TMP[:, i, :][:, 0:T], in_=bank(i)[:, 0:T]),
                                reads=["P%d" % i], writes=["T%d" % i])
                    if ti == 0:
                        sch.add("act", lambda e: e.copy(out=CH[:, 0:hb, :], in_=bank(6)[:, 0:2 * hb].rearrange("p (a b) -> p a b", b=2)),
                                reads=["P6"], writes=["CH"])
                    for i in range(hb):
                        zi = half * hb + i
                        cglob = cxk * hb + i
                        ct = TMP[:, i, :][:, 0:T]
                        sch.add("dve", lambda e, i=i, zi=zi, ct=ct: e.tensor_tensor(
                            out=ZC[:, zi, 2:T + 2], in0=bank(hb + i)[:, 0:T], in1=ct, op=ALU.mult),
                            reads=["P%d" % (hb + i), "T%d" % i], writes=["ZC"])
                        if ti == 0:
                            sch.add("dve", lambda e, i=i, zi=zi: e.tensor_tensor(
                                out=ZC[:, zi, 0:2], in0=bank(6)[:, 2 * (hb + i):2 * (hb + i) + 2], in1=CH[:, i, :], op=ALU.mult),
                                reads=["P6", "CH"], writes=["ZC"])
                        else:
                            sch.add("dve", lambda e, zi=zi, cglob=cglob: e.tensor_copy(out=ZC[:, zi, 0:2], in_=ZSAVE[:, cglob, :]),
                                    reads=["ZSAVE"], writes=["ZC"])
                        sch.add("dve", lambda e, zi=zi, cglob=cglob: e.tensor_copy(out=ZSAVE[:, cglob, :], in_=ZC[:, zi, T:T + 2]),
                                reads=["ZC", "ZSAVE"], writes=["ZSAVE"])
                for ci in range(SC):
                    cb = bj * SC + ci
                    t1 = TMP[:, 2 + ci, :][:, 0:T]
                    w = lambda kk, cb=cb: col(c.c_cw + kk * NCB + cb)
                    sch.add("act", lambda e, ci=ci, t1=t1, w=w: e.mul(out=t1, in_=ZC[:, ci, 0:T], mul=w(0)),
                            reads=["ZC", "COLS"], writes=["T%d" % (2 + ci)])
                    sch.add("dve", lambda e, ci=ci, t1=t1, w=w: e.scalar_tensor_tensor(
                        out=t1, in0=ZC[:, ci, 1:T + 1], scalar=w(1), in1=t1, op0=ALU.mult, op1=ALU.add),
                        reads=["ZC", "COLS", "T%d" % (2 + ci)], writes=["T%d" % (2 + ci)])
                    sch.add("dve", lambda e, ci=ci, t1=t1, w=w: e.scalar_tensor_tensor(
                        out=t1, in0=ZC[:, ci, 2:T + 2], scalar=w(2), in1=t1, op0=ALU.mult, op1=ALU.add),
                        reads=["ZC", "COLS", "T%d" % (2 + ci)], writes=["T%d" % (2 + ci)])
                cB = 2 * c.DA + bj * SW
                proj_fm(win_v, [(cB, SW)], SC, bmode, after_piece1)
                for ci in range(SC):
                    t1 = TMP[:, 2 + ci, :][:, 0:T]
                    sch.add("dve", lambda e, ci=ci, t1=t1: e.tensor_tensor(out=t1, in0=bank(ci)[:, 0:T], in1=t1, op=ALU.mult),
                            reads=["P%d" % ci, "T%d" % (2 + ci)], writes=["T%d" % (2 + ci)])
                for ci in range(SC):
                    cb = bj * SC + ci
                    t1 = TMP[:, 2 + ci, :][:, 0:T]
                    sch.add("act", lambda e, cb=cb, t1=t1: e.mul(out=YT[:, NHA + cb, :], in_=t1, mul=col(c.c_gb + cb)),
                            reads=["T%d" % (2 + ci), "COLS"], writes=[yres(NHA + cb)])
                    sch.add("act", lambda e, t1=t1, ci=ci: e.activation(out=YSQ[:, ci, :], in_=t1, func=AF.Square),
                            reads=["T%d" % (2 + ci)], writes=["YSQ%d" % ci])
                    pend.append(lambda ci=ci, first=(cb == 0): stats_op(ci, 4, 0, first))

            cBc = stcol(TC)
            RB = ST[:, cBc:cBc + TC]
            rb_done = False
            for ds in range(c.DS):
                d0 = ds * c.DSW
                for part, (cc0, ncc, RR, rname) in enumerate(((0, NHA, RA, "RA"), (NHA, NCB, RB, "RB"))):
                    npq = ncc // KP
                    for q in range(npq):
                        si = next_slot("late")
                        s3 = slot3(si, KP, c.DSW)
                        weight_dma(si, [(s3, wout_v[:, cc0 + q * KP:cc0 + (q + 1) * KP, d0:d0 + c.DSW])])
                        for tc in range(TC):
                            def f(e, q=q, tc=tc, s3=s3, cc0=cc0, npq=npq):
                                r = None
                                for j in range(KP):
                                    r = e.matmul(out=bank(tc)[:, 0:c.DSW], lhsT=YT[:, cc0 + q * KP + j, tc * 128:(tc + 1) * 128],
                                                 rhs=s3[:, j, :], start=(q == 0 and j == 0),
                                                 stop=(q == npq - 1 and j == KP - 1))
                                return r
                            yr = sorted(set(yres(cc0 + q * KP + j) for j in range(KP)))
                            sch.add("pe", f, reads=[slot_res[si]] + yr, writes=["P%d" % tc])
                        if not rb_done:
                            rb_done = True
                            flush_pend(0)
                            rsqrt_ops(RB, bank(4)[:, 0:TC], 1.0 / c.DB, ["P4"], "RB")
                    for tc in range(TC):
                        sch.add("dve", lambda e, tc=tc, RR=RR, d0=d0: e.scalar_tensor_tensor(
                            out=XRES[:, tc, d0:d0 + c.DSW], in0=bank(tc)[:, 0:c.DSW], scalar=RR[:, tc:tc + 1],
                            in1=XRES[:, tc, d0:d0 + c.DSW], op0=ALU.mult, op1=ALU.add),
                            reads=["P%d" % tc, rname] + xr(tc), writes=xr(tc))

            emit_norm(ti, 1, "ffn")

            NG = (FC + G - 1) // G
            dbank = {"i": 0}
            deferred_down = []

            def emit_down(gj, nchg, k):
                for ds in range(c.DS):
                    d0 = ds * c.DSW
                    si = next_slot("late")
                    s3 = slot3(si, nchg, c.DSW)
                    weight_dma(si, [(s3, wd_v[:, gj * G:gj * G + nchg, d0:d0 + c.DSW])])
                    for tc in range(TC):
                        b = 4 + (dbank["i"] % 4); dbank["i"] += 1
                        def f(e, tc=tc, s3=s3, b=b, nchg=nchg, k=k):
                            r = None
                            for gi in range(nchg):
                                r = e.matmul(out=bank(b)[:, 0:c.DSW], lhsT=AT[k][:, gi, tc * 128:(tc + 1) * 128],
                                             rhs=s3[:, gi, :], start=(gi == 0), stop=(gi == nchg - 1))
                            return r
                        sch.add("pe", f, reads=[slot_res[si], ATRES[k]], writes=["P%d" % b])
                        sch.add("dve", lambda e, tc=tc, b=b, d0=d0: e.tensor_tensor(
                            out=XRES[:, tc, d0:d0 + c.DSW], in0=bank(b)[:, 0:c.DSW], in1=XRES[:, tc, d0:d0 + c.DSW], op=ALU.add),
                            reads=["P%d" % b] + xr(tc), writes=xr(tc))

            for gj in range(NG):
                f0 = gj * G
                nchg = min(G, FC - f0)
                k = gj % 2
                for s0 in range(0, nchg, SC):
                    nch = min(SC, nchg - s0)
                    cF = (f0 + s0) * 128
                    proj_fm(wg_v, [(cF, nch * 128)], nch, "late")
                    for ci in range(nch):
                        sch.add("act", lambda e, ci=ci: e.activation(out=SG[:, ci, :], in_=bank(ci)[:, 0:T], func=AF.Silu),
                                reads=["P%d" % ci], writes=["SG%d" % ci])
                    proj_fm(wu_v, [(cF, nch * 128)], nch, "late")
                    for ci in range(nch):
                        sch.add("dve", lambda e, ci=ci, k=k, s0=s0: e.tensor_tensor(
                            out=AT[k][:, s0 + ci, :], in0=bank(ci)[:, 0:T], in1=SG[:, ci, :], op=ALU.mult),
                            reads=["P%d" % ci, "SG%d" % ci], writes=[ATRES[k]])
                    if s0 == 0 and deferred_down:
                        deferred_down.pop(0)()
                deferred_down.append(lambda gj=gj, nchg=nchg, k=k: emit_down(gj, nchg, k))
                if gj == 0:
                    pass
            while deferred_down:
                deferred_down.pop(0)()

            sch.add("sp", lambda e: [e.dma_start(out=GB, in_=gains_d[2].partition_broadcast(128))],
                    writes=["YC"], dma_key="gb")
            finfo = {}
            allht = [r for q in range(NPQ) for r in ht_res(q)]
            for tc in range(TC):
                ci = stcol(2)
                ss, rr = ST[:, ci:ci + 1], ST[:, ci + 1:ci + 2]
                sres = "sf%d" % ci
                finfo[tc] = (rr, sres)
                sch.add("act", lambda e, tc=tc, ss=ss: e.activation(out=HTflat[:, 0:D], in_=XRES[:, tc, :], func=AF.Square,
                                                                    accum_out=ss),
                        reads=xr(tc), writes=allht + [sres])
                rsqrt_ops(rr, ss, 1.0 / D, [sres], sres + "r")
            for tc in range(TC):
                rr, sres = finfo[tc]
                sch.add("dve", lambda e, tc=tc, rr=rr: e.scalar_tensor_tensor(
                    out=XRES[:, tc, :], in0=XRES[:, tc, :], scalar=rr, in1=GB, op0=ALU.mult, op1=ALU.mult),
                    reads=xr(tc) + [sres + "r", "YC"], writes=xr(tc))
                sch.add("sp", lambda e, tc=tc, r0=r0: [e.dma_start(out=out_d[r0 + tc * 128:r0 + (tc + 1) * 128, :], in_=XRES[:, tc, :])],
                        reads=xr(tc), dma_key="o%d" % tc)

        sems = {}
        for n in sch.sem_names():
            sems[n] = es.enter_context(nc.semaphore(n.replace(":", "_")))
        with nc.allow_low_precision("bf16 matmul operands, fp32 PSUM accumulation"):
            with nc.Block() as block:
                @block.tensor
                def _(e):
                    sch.emit("pe", e, sems)

                @block.scalar
                def _(e):
                    sch.emit("act", e, sems)

                @block.vector
                def _(e):
                    sch.emit("dve", e, sems)

                @block.gpsimd
                def _(e):
                    sch.emit("pool", e, sems)

                @block.sync
                def _(e):
                    sch.emit("sp", e, sems)
                    for tc in range(TC):
                        e.wait_ge(sems["d:o%d" % tc], 16 * sch.dcnt["o%d" % tc])
    return nc


def make_in_maps(cfg, x2d, p):
    c = cfg
    rows = c.NTILE * c.T
    cols = np.zeros((128, c.NCOLS), np.float32)
    cols[:, c.c_lnvg:c.c_lnvg + c.NHA] = p["ln_v_g"].reshape(c.NHA, 128).T
    cols[:, c.c_lnvb:c.c_lnvb + c.NHA] = p["ln_v_b"].reshape(c.NHA, 128).T
    cols[:, c.c_ga:c.c_ga + c.NHA] = p["out_norm_a_g"].reshape(c.NHA, 128).T
    cols[:, c.c_gb:c.c_gb + c.NCB] = p["out_norm_b_g"].reshape(c.NCB, 128).T
    cols[:, c.c_cw:c.c_cw + 3 * c.NCB] = p["conv_w"].reshape(3, c.NCB, 128).transpose(2, 0, 1).reshape(128, 3 * c.NCB)
    cols[:, c.c_gmix:c.c_gmix + c.DC] = p["mix_norm_g"].reshape(c.DC, 128).T
    gains = np.ascontiguousarray(np.stack([p["mix_norm_g"], p["ffn_norm_g"], p["final_norm_g"]]).astype(np.float32))
    shared = {
        "cols": cols, "gains": gains,
        "w_spatial": np.ascontiguousarray(p["w_spatial"], dtype=np.float32),
        "b_spatial": np.ascontiguousarray(p["b_spatial"].reshape(1, -1), dtype=np.float32),
        "w_in": np.ascontiguousarray(p["w_in"], dtype=np.float32),
        "w_out": np.ascontiguousarray(p["w_out"], dtype=np.float32),
        "w_gate": np.ascontiguousarray(p["w_gate"], dtype=np.float32),
        "w_up": np.ascontiguousarray(p["w_up"], dtype=np.float32),
        "w_down": np.ascontiguousarray(p["w_down"], dtype=np.float32),
    }
    in_maps = []
    for i in range(c.NCORES):
        xs = np.ascontiguousarray(x2d[i * rows:(i + 1) * rows])
        halo = np.zeros((2, c.D), np.float32)
        if i > 0:
            halo[:] = x2d[i * rows - 2:i * rows]
        xh = np.ascontiguousarray(halo.reshape(2, c.DC, 128).transpose(2, 1, 0))
        m = dict(shared)
        m["x"] = xs
        m["xh"] = xh
        in_maps.append(m)
    return in_maps


_PROGRAM_CACHE = {}


def kernel(x, mix_norm_g, w_in, ln_v_g, ln_v_b, w_spatial, b_spatial, conv_w,
           out_norm_a_g, out_norm_b_g, w_out, ffn_norm_g, w_gate, w_up, w_down, final_norm_g):
    cfg = Cfg()
    x = np.asarray(x, dtype=np.float32)
    B, S, D = x.shape
    assert D == cfg.D and B * S == cfg.NCORES * cfg.NTILE * cfg.T
    p = {
        "mix_norm_g": np.asarray(mix_norm_g)[0], "w_in": np.asarray(w_in)[0],
        "ln_v_g": np.asarray(ln_v_g)[0], "ln_v_b": np.asarray(ln_v_b)[0],
        "w_spatial": np.asarray(w_spatial)[0], "b_spatial": np.asarray(b_spatial)[0],
        "conv_w": np.asarray(conv_w)[0], "out_norm_a_g": np.asarray(out_norm_a_g)[0],
        "out_norm_b_g": np.asarray(out_norm_b_g)[0], "w_out": np.asarray(w_out)[0],
        "ffn_norm_g": np.asarray(ffn_norm_g)[0], "w_gate": np.asarray(w_gate)[0],
        "w_up": np.asarray(w_up)[0], "w_down": np.asarray(w_down)[0],
        "final_norm_g": np.asarray(final_norm_g),
    }
    in_maps = make_in_maps(cfg, x.reshape(B * S, D), p)
    if "nc" not in _PROGRAM_CACHE:
        _PROGRAM_CACHE["nc"] = build_program(cfg)
    nc = _PROGRAM_CACHE["nc"]
    res = run_bass_kernel_spmd(nc, in_maps, core_ids=list(range(cfg.NCORES)))
    out = np.concatenate([np.asarray(r["out"]) for r in res.results], axis=0)
    return out.reshape(B, S, D).astype(np.float32)
```
